# Optimizing a Trainium2 kernel written in Bass

```python
import math
import jax, jax.numpy as jnp
from jax import lax
import numpy as np

D_MODEL = 1024
BATCH = 4
SEQ = 8192
DEPTH = 2

N_MIXERS = 2
N_POOL_LAYERS = (DEPTH + 1) // 2
N_MOBA_LAYERS = DEPTH // 2
POOL_WINDOWS = (2, 4, 8, 16)
N_POOL_GROUPS = len(POOL_WINDOWS)
POOL_GROUP = D_MODEL // N_POOL_GROUPS
HEAD_DIM = 64
N_HEADS = D_MODEL // HEAD_DIM
MOBA_BLOCK = 256
MOBA_TOPK = 3
Q_CHUNK = 128
NUM_BUCKETS = 32
MAX_DISTANCE = 1024
D_FF = 4 * D_MODEL
EPS = 1e-6

kernel_name = "hybrid_pool_moba_adaln_block"


def rmsnorm(x, g):
    xf = x.astype(jnp.float32)
    y = xf * lax.rsqrt(jnp.mean(xf * xf, axis=-1, keepdims=True) + EPS)
    return (y * g.astype(jnp.float32)).astype(x.dtype)


def modulate(h, shift, scale):
    return h * (1 + scale[:, None, :]) + shift[:, None, :]


def t5_bucket(dist):
    max_exact = NUM_BUCKETS // 2
    nf = jnp.maximum(dist, 1).astype(jnp.float32)
    large = max_exact + (jnp.log(nf / max_exact) / math.log(MAX_DISTANCE / max_exact)
                         * (NUM_BUCKETS - max_exact)).astype(jnp.int32)
    large = jnp.minimum(large, NUM_BUCKETS - 1)
    return jnp.where(dist < max_exact, dist, large)


def pool_mixer(h, w_pool, pool_scale):
    B, S, D = h.shape
    hf = h.astype(jnp.float32)
    cs = jnp.concatenate([jnp.zeros((B, 1, D), jnp.float32), jnp.cumsum(hf, axis=1)], axis=1)
    t = jnp.arange(S)
    groups = []
    for g, w in enumerate(POOL_WINDOWS):
        sl = slice(g * POOL_GROUP, (g + 1) * POOL_GROUP)
        cs_g = cs[:, :, sl]
        lo = jnp.maximum(t + 1 - w, 0)
        win_sum = cs_g[:, 1:] - jnp.take(cs_g, lo, axis=1)
        cnt = jnp.minimum(t + 1, w).astype(jnp.float32)[None, :, None]
        groups.append(win_sum / cnt - hf[:, :, sl])
    pooled = jnp.stack(groups, axis=2)
    y = jnp.einsum('bsgi,gio->bsgo', pooled, w_pool.astype(jnp.float32)).reshape(B, S, D)
    return (y * pool_scale.astype(jnp.float32)).astype(h.dtype)


def moba_attention(h, w_qkv, w_o, rel_bias):
    B, S, D = h.shape
    H, Dh, BL = N_HEADS, HEAD_DIM, MOBA_BLOCK
    qkv = h @ w_qkv
    q, k, v = jnp.split(qkv, 3, axis=-1)
    q = q.reshape(B, S, H, Dh).transpose(0, 2, 1, 3)
    k = k.reshape(B, S, H, Dh).transpose(0, 2, 1, 3)
    v = v.reshape(B, S, H, Dh).transpose(0, 2, 1, 3)
    nb = -(-S // BL)
    pad = nb * BL - S
    kb = jnp.pad(k, ((0, 0), (0, 0), (0, pad), (0, 0))).reshape(B, H, nb, BL, Dh)
    vb = jnp.pad(v, ((0, 0), (0, 0), (0, pad), (0, 0))).reshape(B, H, nb, BL, Dh)
    kmean = jnp.mean(kb.astype(jnp.float32), axis=3)
    top = min(MOBA_TOPK, nb - 1)
    scale = HEAD_DIM ** -0.5
    rb = rel_bias.astype(jnp.float32)
    rbT = rb.T
    bi = jnp.arange(B)[:, None, None]
    hi = jnp.arange(H)[None, :, None]
    hi4 = jnp.arange(H)[None, :, None, None]
    n_chunks = S // Q_CHUNK

    def chunk_fn(ci):
        s0 = ci * Q_CHUNK
        qf = lax.dynamic_slice_in_dim(q, s0, Q_CHUNK, axis=2).astype(jnp.float32)
        tq = s0 + jnp.arange(Q_CHUNK)
        own = s0 // BL
        k_own = lax.dynamic_index_in_dim(kb, own, axis=2, keepdims=False)
        v_own = lax.dynamic_index_in_dim(vb, own, axis=2, keepdims=False)
        dist_own = tq[:, None] - (own * BL + jnp.arange(BL))[None, :]
        bias_own = rb[t5_bucket(jnp.maximum(dist_own, 0))].transpose(2, 0, 1)[None]
        logit_own = jnp.einsum('bhqd,bhkd->bhqk', qf, k_own.astype(jnp.float32)) * scale + bias_own
        logit_own = jnp.where((dist_own >= 0)[None, None], logit_own, -jnp.inf)
        logits = []
        sel = []
        if top > 0:
            gate = jnp.einsum('bhqd,bhnd->bhqn', qf, kmean)
            past = jnp.arange(nb) < own
            gate = jnp.where(past[None, None, None], gate, -jnp.inf)
            _, idx = lax.top_k(gate, top)
            for r in range(top):
                idx_r = idx[..., r]
                k_sel = kb[bi, hi, idx_r]
                kpos = idx_r[..., None] * BL + jnp.arange(BL)
                dist = tq[None, None, :, None] - kpos
                bias_r = rbT[hi4, t5_bucket(jnp.maximum(dist, 0))]
                lg = jnp.einsum('bhqd,bhqkd->bhqk', qf, k_sel.astype(jnp.float32)) * scale + bias_r
                lg = jnp.where((idx_r < own)[..., None], lg, -jnp.inf)
                logits.append(lg)
                sel.append(idx_r)
        logits.append(logit_own)
        p = jax.nn.softmax(jnp.concatenate(logits, axis=-1), axis=-1)
        out = jnp.einsum('bhqk,bhkd->bhqd', p[..., top * BL:], v_own.astype(jnp.float32))
        for r in range(top):
            v_sel = vb[bi, hi, sel[r]]
            out = out + jnp.einsum('bhqk,bhqkd->bhqd', p[..., r * BL:(r + 1) * BL],
                                   v_sel.astype(jnp.float32))
        return out.astype(h.dtype)

    o = lax.map(chunk_fn, jnp.arange(n_chunks))
    o = o.transpose(1, 0, 3, 2, 4).reshape(B, S, D)
    return o @ w_o


def squared_relu_mlp(h, w_up, w_down):
    a = jax.nn.relu(h @ w_up)
    return (a * a) @ w_down


def setup_inputs(seed: int = 0) -> dict:
    key = jax.random.key(seed)
    ks = jax.random.split(key, 16)
    f32 = jnp.float32
    D = D_MODEL
    nrm = lambda k, shape, s: jax.random.normal(k, shape, f32) * s
    return {
        "x": nrm(ks[0], (BATCH, SEQ, D), 1.0),
        "c": nrm(ks[1], (BATCH, D), 1.0),
        "rel_bias": nrm(ks[2], (NUM_BUCKETS, N_HEADS), 0.5),
        "w_mod": nrm(ks[3], (DEPTH, D, 6 * D), 0.5 * D ** -0.5),
        "b_mod": nrm(ks[4], (DEPTH, 6 * D), 0.02),
        "norm_mix": 1.0 + nrm(ks[5], (DEPTH, D), 0.02),
        "norm_mlp": 1.0 + nrm(ks[6], (DEPTH, D), 0.02),
        "w_pool": nrm(ks[7], (N_POOL_LAYERS, N_POOL_GROUPS, POOL_GROUP, POOL_GROUP), POOL_GROUP ** -0.5),
        "pool_scale": 1.0 + nrm(ks[8], (N_POOL_LAYERS, D), 0.05),
        "w_qkv": nrm(ks[9], (N_MOBA_LAYERS, D, 3 * D), D ** -0.5),
        "w_o": nrm(ks[10], (N_MOBA_LAYERS, D, D), D ** -0.5),
        "w_up": nrm(ks[11], (DEPTH, D, D_FF), D ** -0.5),
        "w_down": nrm(ks[12], (DEPTH, D_FF, D), D_FF ** -0.5),
        "norm_final": 1.0 + nrm(ks[13], (D,), 0.02),
    }


def reference(x, c, rel_bias, w_mod, b_mod, norm_mix, norm_mlp, w_pool, pool_scale,
              w_qkv, w_o, w_up, w_down, norm_final):
    c_act = jax.nn.silu(c)
    for i in range(DEPTH):
        mod = c_act @ w_mod[i] + b_mod[i]
        sh1, sc1, g1, sh2, sc2, g2 = jnp.split(mod, 6, axis=-1)
        h = modulate(rmsnorm(x, norm_mix[i]), sh1, sc1)
        if i % N_MIXERS == 0:
            y = pool_mixer(h, w_pool[i // N_MIXERS], pool_scale[i // N_MIXERS])
        else:
            y = moba_attention(h, w_qkv[i // N_MIXERS], w_o[i // N_MIXERS], rel_bias)
        x = x + g1[:, None, :] * y
        h = modulate(rmsnorm(x, norm_mlp[i]), sh2, sc2)
        x = x + g2[:, None, :] * squared_relu_mlp(h, w_up[i], w_down[i])
    return rmsnorm(x, norm_final)
```

```python
import os
import numpy as np
from contextlib import ExitStack
import concourse.bass as bass
import concourse.mybir as mybir
from concourse.bass_utils import run_bass_kernel_spmd


ENGS = ['pe', 'act', 'dve', 'pool', 'sp']


class Sched:
    def __init__(self, nc, es, same_sync=True):
        self.nc = nc
        self.es = es
        self.same_sync = same_sync
        self.ops = {e: [] for e in ENGS}
        self.sem = {e: es.enter_context(nc.semaphore("s_" + e)) for e in ENGS}
        self.cnt = {e: 0 for e in ENGS}
        self.seen = {e: {} for e in ENGS}
        self.lastw = {}
        self.readers = {}
        self.dsem = {}
        self.dcnt = {}

    def _handle(self, k):
        return self.sem[k] if k in self.sem else self.dsem[k]

    def _deps(self, eng, reads, writes):
        deps = {}

        def add(ev):
            k, v = ev
            if k == eng and (eng == 'pe' or not self.same_sync):
                return
            if deps.get(k, 0) < v:
                deps[k] = v
        for t in reads:
            if t in self.lastw:
                add(self.lastw[t])
        for t in writes:
            if t in self.lastw:
                add(self.lastw[t])
            for ev in self.readers.get(t, ()):
                add(ev)
        out = []
        for k, v in deps.items():
            if self.seen[eng].get(k, 0) < v:
                self.seen[eng][k] = v
                out.append((k, v))
        return out

    def _record(self, ev, reads, writes):
        for t in reads:
            self.readers.setdefault(t, []).append(ev)
        for t in writes:
            self.lastw[t] = ev
            self.readers[t] = []

    def op(self, eng, fn, reads=(), writes=(), track=True):
        assert track or eng == 'pe'
        waits = self._deps(eng, reads, writes)
        ev = (eng, self.cnt[eng] + 1)
        if track:
            self.cnt[eng] += 1
        self.ops[eng].append((waits, fn, eng if track else None, 1))
        self._record(ev, reads, writes)

    def raw(self, eng, fn):
        self.ops[eng].append(([], fn, None, 0))

    def custom(self, q, fn, reads=(), writes=(), key=None):
        assert key not in self.dsem
        self.dsem[key] = self.es.enter_context(self.nc.semaphore("c_" + str(key)))
        waits = self._deps(q, reads, writes)
        self.dcnt[key] = 1
        self.ops[q].append((waits, fn, key, None))
        self._record((key, 1), reads, writes)

    def barrier(self):
        for e in ENGS:
            waits = []
            for f in ENGS:
                if f != e and self.cnt[f] > self.seen[e].get(f, 0):
                    waits.append((f, self.cnt[f]))
                    self.seen[e][f] = self.cnt[f]
            for k, v in self.dcnt.items():
                if self.seen[e].get(k, 0) < v:
                    waits.append((k, v))
                    self.seen[e][k] = v
            self.ops[e].append((waits, None, None, 0))
        self.lastw.clear()
        self.readers.clear()

    def dma(self, q, out, in_, reads=(), writes=(), key=None, in_fn=None, **kw):
        if key not in self.dsem:
            self.dsem[key] = self.es.enter_context(self.nc.semaphore("d_" + str(key)))
            self.dcnt[key] = 0
        waits = self._deps(q, reads, writes)
        self.dcnt[key] += 16
        ev = (key, self.dcnt[key])
        if in_fn is None:
            self.ops[q].append((waits, (lambda e: e.dma_start(out=out, in_=in_, **kw)), key, 16))
        else:
            self.ops[q].append((waits, (lambda e: e.dma_start(out=out, in_=in_fn(), **kw)), key, 16))
        self._record(ev, reads, writes)

    def wait_all(self, eng, keys):
        waits = [(k, self.dcnt[k]) for k in keys]
        self.ops[eng].append((waits, None, None, 0))

    def emit(self):
        nc = self.nc
        self.wait_all('sp', list(self.dcnt.keys()))
        with nc.Block() as block:
            def run(name):
                def body(e):
                    for waits, fn, inc, amt in self.ops[name]:
                        for k, v in waits:
                            e.wait_ge(self._handle(k), v)
                        if fn is None:
                            continue
                        ins = fn(e)
                        if inc is not None:
                            if amt is None:
                                ins.then_inc(self._handle(inc))
                            else:
                                ins.then_inc(self._handle(inc), amt)
                return body
            block.tensor(run('pe'))
            block.scalar(run('act'))
            block.vector(run('dve'))
            block.gpsimd(run('pool'))
            block.sync(run('sp'))
        self.ops = {e: [] for e in ENGS}


F32 = mybir.dt.float32
BF16 = mybir.dt.bfloat16
ALU = mybir.AluOpType
AF = mybir.ActivationFunctionType

D = 1024
DFF = 4096
NT = 256
NBLK = 16
TOK = NT * NBLK
HALO = 16
EPS = 1e-6
BIG = 30000.0


class Ctx:
    def __init__(self, nc, es):
        self.nc = nc
        self.es = es
        self.S = Sched(nc, es)
        self.psn = 0
        self.banks = []
        self.uid = 0
        self.prefix = ""
        self.ges = es
        self.castn = 0

    def sb(self, name, shape, dt):
        return self.es.enter_context(self.nc.sbuf_tensor("sb_" + self.prefix + name, shape, dt))

    def init_psum(self, pools=None):
        if not self.banks:
            for i in range(8):
                self.banks.append(self.ges.enter_context(self.nc.psum_tensor(f"psb{i}", [128, 512], F32)))
        self.pools = pools or {'main': list(range(8))}
        self.pcnt = {k: 0 for k in self.pools}

    def ps(self, n=512, pool='main'):
        ids = self.pools[pool]
        i = ids[self.pcnt[pool] % len(ids)]
        self.pcnt[pool] += 1
        return self.banks[i][:, 0:n], f"ps{i}"

    def tok(self, base):
        self.uid += 1
        return f"{base}#{self.uid}"


class Fz:
    def __init__(self, nc, C, T):
        self.nc, self.C, self.T = nc, C, T


def _begin(fz, es, pools=None, prefix=""):
    if fz is None:
        nc = bass.Bass("TRN2", target_bir_lowering=False)
        C = Ctx(nc, es)

        def decl(name, shape, dtype, kind=None):
            if kind is None:
                return nc.dram_tensor(name, shape, dtype).ap()
            return nc.dram_tensor(name, shape, dtype, kind=kind).ap()
    else:
        nc, C = fz.nc, fz.C
        C.es = es
        C.prefix = prefix

        def decl(name, shape, dtype, kind=None):
            return fz.T.get(prefix + name)
    C.init_psum(pools)
    return nc, C, decl


def _end(fz, C):
    if fz is not None:
        C.S.barrier()
    C.S.emit()


def emit_consts(C):
    S = C.S
    C.ones_bf = C.sb("ones_bf", [128, 128], BF16)
    S.op('pool', lambda e: e.memset(C.ones_bf[:], 1.0), writes=['ones_bf'])


def emit_mod(C, cT_d, wmod_d, bmod_d, groups, out_tile, stage):
    S = C.S
    cact = C.sb(C.tok("cact"), [128, 8], F32)
    bm = C.sb(C.tok("bm"), [128, 48], F32)
    ctok = C.tok("cact")
    btok = C.tok("bm")
    S.dma('sp', cact[:], cT_d[:, :], writes=[ctok], key=ctok)
    S.dma('sp', bm[:], bmod_d[:, :], writes=[btok], key=btok)
    S.op('act', lambda e: e.activation(out=cact[:], in_=cact[:], func=AF.Silu), reads=[ctok], writes=[ctok])
    pi = 0
    for gi, g in enumerate(groups):
        ps, pst = C.ps(512)
        for dc in range(8):
            st, sttoks = stage[pi % 2]
            pi += 1
            S.dma('sp' if pi % 2 else 'act', st[:], wmod_d[dc * 128:(dc + 1) * 128, g * 1024:(g + 1) * 1024], writes=sttoks, key=sttoks[0] + '_m')
            for mch in range(8):
                S.op('pe', (lambda e, ps=ps, st=st, dc=dc, mch=mch: e.matmul(
                    ps[:, mch:mch + 1], lhsT=st[:, mch * 128:(mch + 1) * 128], rhs=cact[:, dc:dc + 1],
                    start=(dc == 0 and mch == 0), stop=(dc == 7 and mch == 7))),
                    reads=sttoks + [ctok], writes=[pst], track=(mch == 7))
        S.op('dve', (lambda e, ps=ps, gi=gi, g=g: e.tensor_tensor(
            out=out_tile[:, gi, :], in0=ps[:, 0:8], in1=bm[:, g * 8:(g + 1) * 8], op=ALU.add)),
            reads=[pst, btok], writes=['mod'])


def emit_cast_weight(C, dst, dsttok, src_views, stage, engines=('dve', 'act', 'dve', 'pool')):
    S = C.S
    quarters = []
    for st, toks in stage:
        quarters.append((st[:, 0:512], toks[0]))
        quarters.append((st[:, 512:1024], toks[1]))
    pieces = []
    for (src, dview, kind) in src_views:
        if kind == '2d':
            for hh in range(2):
                pieces.append((src[:, hh * 512:(hh + 1) * 512], dview[:, hh * 512:(hh + 1) * 512], None))
        else:
            pieces.append((src, dview, kind))
    for (src, dview, kind) in pieces:
        n = C.castn
        C.castn += 1
        qv, qtok = quarters[n % 4]
        sv = qv if kind is None else kind(qv)
        S.dma('sp' if n % 2 == 0 else 'act', sv, src, writes=[qtok], key=qtok)
        eng = engines[n % len(engines)]
        if eng == 'act':
            S.op('act', (lambda e, dview=dview, sv=sv: e.activation(out=dview, in_=sv, func=AF.Copy)), reads=[qtok], writes=[dsttok])
        else:
            S.op(eng, (lambda e, dview=dview, sv=sv: e.tensor_copy(out=dview, in_=sv)), reads=[qtok], writes=[dsttok])


def load_ffn_weights(C, wup_d, wdn_d, stage):
    wup = C.sb("wup_bf", [128, 8, DFF], BF16)
    wdn = C.sb("wdn_bf", [128, 32, D], BF16)
    views = []
    for dc in range(8):
        for q in range(4):
            src = wup_d[dc * 128:(dc + 1) * 128, q * 1024:(q + 1) * 1024]
            views.append((src, wup[:, dc, q * 1024:(q + 1) * 1024], '2d'))
    emit_cast_weight(C, wup, "wup_bf", views, stage)
    views = []
    for fc in range(32):
        src = wdn_d[fc * 128:(fc + 1) * 128, :]
        views.append((src, wdn[:, fc, :], '2d'))
    emit_cast_weight(C, wdn, "wdn_bf", views, stage)
    return wup, wdn


def emit_norm(C, xt, xtok, ncols, A, B, col, out_fn, bufs, rtag):
    for _ in emit_norm_gen(C, xt, xtok, ncols, A, B, col, out_fn, bufs, rtag):
        pass


def emit_norm_gen(C, xt, xtok, ncols, A, B, col, out_fn, bufs, rtag):
    S = C.S
    xsq, rstd = bufs['xsq'], bufs['rstd']
    sq = rstd
    for c in range(8):
        S.op('act', (lambda e, c=c: e.activation(out=xsq[:, c, 0:ncols], in_=xt[:, c, 0:ncols], func=AF.Square)),
             reads=[xtok], writes=['xsq'])
    yield
    ps, pst = C.ps(512)
    for c in range(8):
        S.op('pe', (lambda e, c=c: e.matmul(ps[:, 0:ncols], lhsT=C.ones_bf[:], rhs=xsq[:, c, 0:ncols],
                                            start=(c == 0), stop=(c == 7))),
             reads=['xsq', 'ones_bf'], writes=[pst], track=(c == 7))
    S.op('act', lambda e: e.activation(out=sq[:, 0:ncols], in_=ps[:, 0:ncols], func=AF.Sqrt, bias=EPS, scale=1.0 / D),
         reads=[pst], writes=['rstd'])
    S.op('dve', lambda e: e.reciprocal(out=rstd[:, 0:ncols], in_=sq[:, 0:ncols]), reads=['rstd'], writes=['rstd'])
    for c in range(8):
        tmp, ttok = bufs['tmp'][c % 2]
        S.op('dve', (lambda e, c=c, tmp=tmp: e.tensor_tensor(out=tmp[:, 0:ncols], in0=xt[:, c, 0:ncols],
                                                               in1=rstd[:, 0:ncols], op=ALU.mult)),
             reads=[xtok, 'rstd'], writes=[ttok])
        o, otok = out_fn(c)
        S.op('act', (lambda e, c=c, tmp=tmp, o=o: e.activation(out=o, in_=tmp[:, 0:ncols], func=AF.Identity,
                                                                bias=B[:, col[1] + c:col[1] + c + 1],
                                                                scale=A[:, col[0] + c:col[0] + c + 1])),
             reads=[ttok, 'modc'], writes=[otok])


def ffn_up_gen(C, wup, h2, a2, rl):
    S = C.S
    for f2 in range(16):
        ps, pst = C.ps(512)
        for k in range(2):
            fc = 2 * f2 + k
            for dc in range(8):
                S.op('pe', (lambda e, ps=ps, dc=dc, fc=fc, k=k: e.matmul(
                    ps[:, k * NT:(k + 1) * NT], lhsT=wup[:, dc, fc * 128:(fc + 1) * 128],
                    rhs=h2[:, dc, :], start=(dc == 0), stop=(dc == 7))),
                    reads=['h2', 'wup_bf'], writes=[pst], track=(dc == 7 and k == 1))
        r, rtok = rl[f2 % 2]
        S.op('act', (lambda e, ps=ps, r=r: e.activation(out=r, in_=ps, func=AF.Relu)), reads=[pst], writes=[rtok])
        S.op('dve', (lambda e, r=r, f2=f2: e.tensor_tensor(
            out=a2[:, 2 * f2:2 * f2 + 2, :], in0=r.rearrange("p (a b) -> p a b", a=2),
            in1=r.rearrange("p (a b) -> p a b", a=2), op=ALU.mult)),
            reads=[rtok], writes=[f'a2_{f2}'])
        yield


def ffn_down_gen(C, xt, xtok, xoff, wdn, a2, G2, g2col):
    S = C.S
    for d2 in range(4):
        ps, pst = C.ps(512)
        for k in range(2):
            dc = 2 * d2 + k
            for fc in range(32):
                S.op('pe', (lambda e, ps=ps, dc=dc, fc=fc, k=k: e.matmul(
                    ps[:, k * NT:(k + 1) * NT], lhsT=wdn[:, fc, dc * 128:(dc + 1) * 128],
                    rhs=a2[:, fc, :], start=(fc == 0), stop=(fc == 31))),
                    reads=[f'a2_{fc // 2}', 'wdn_bf'], writes=[pst], track=(fc == 31 and k == 1))
            yield
        for k in range(2):
            dc = 2 * d2 + k
            S.op('dve', (lambda e, ps=ps, dc=dc, k=k: e.scalar_tensor_tensor(
                out=xt[:, dc, xoff:xoff + NT], in0=ps[:, k * NT:(k + 1) * NT], scalar=G2[:, g2col + dc:g2col + dc + 1],
                in1=xt[:, dc, xoff:xoff + NT], op0=ALU.mult, op1=ALU.add)),
                reads=[pst, xtok, 'modc'], writes=[xtok])
        yield


def interleave(main, side, every=1):
    n = 0
    for _ in main:
        n += 1
        if side is not None and n % every == 0:
            if next(side, 'done') == 'done':
                side = None
    if side is not None:
        for _ in side:
            pass


def build_p1(ntiles=NBLK, stop=99, fz=None):
    with ExitStack() as es:
        nc, C, decl = _begin(fz, es, None, "p1_")
        xTh = decl("xTh", [D, NBLK, HALO + NT], F32, "ExternalInput")
        cT = decl("cT", [128, 8], F32, "ExternalInput")
        wmod = decl("wmod", [D, 6 * D], F32, "ExternalInput")
        bmod = decl("bmod", [128, 48], F32, "ExternalInput")
        nrm = decl("nrm", [128, 24], F32, "ExternalInput")
        hval = decl("hval", [128, NBLK], F32, "ExternalInput")
        invc = decl("invc", [128, 8, HALO], F32, "ExternalInput")
        wpool = decl("wpool", [4, 256, 256], F32, "ExternalInput")
        wup_d = decl("wup", [D, DFF], F32, "ExternalInput")
        wdn_d = decl("wdn", [DFF, D], F32, "ExternalInput")
        x1T = decl("x1T", [D, TOK], F32, "ExternalOutput")
        W = HALO + NT
        S = C.S
        emit_consts(C)
        stage = [(C.sb("stg0", [128, 1024], F32), ["stg0a", "stg0b"]), (C.sb("stg1", [128, 1024], F32), ["stg1a", "stg1b"])]
        mod = C.sb("mod", [128, 6, 8], F32)
        nrm_t = C.sb("nrm_t", [128, 24], F32)
        hv = C.sb("hv", [128, NBLK], F32)
        ic = C.sb("ic", [128, 8, HALO], F32)
        S.dma('sp', nrm_t[:], nrm[:, :], writes=['nrm_t'], key='ld_nrm')
        S.dma('sp', hv[:], hval[:, :], writes=['hv'], key='ld_hv')
        S.dma('sp', ic[:], invc[:, :, :], writes=['ic'], key='ld_ic')
        emit_mod(C, cT, wmod, bmod, [0, 1, 2, 3, 4, 5], mod, stage)
        modc = C.sb("modc", [128, 48], F32)
        mt = 'mod'
        S.op('dve', lambda e: e.scalar_tensor_tensor(out=modc[:, 0:8], in0=mod[:, 1, :], scalar=1.0, in1=nrm_t[:, 0:8],
                                                     op0=ALU.add, op1=ALU.mult), reads=[mt, 'nrm_t'], writes=['modc'])
        S.op('dve', lambda e: e.tensor_copy(out=modc[:, 8:16], in_=mod[:, 0, :]), reads=[mt, 'modc'], writes=['modc'])
        S.op('dve', lambda e: e.tensor_tensor(out=modc[:, 16:24], in0=mod[:, 2, :], in1=nrm_t[:, 16:24], op=ALU.mult),
             reads=[mt, 'nrm_t', 'modc'], writes=['modc'])
        S.op('dve', lambda e: e.scalar_tensor_tensor(out=modc[:, 24:32], in0=mod[:, 4, :], scalar=1.0, in1=nrm_t[:, 8:16],
                                                     op0=ALU.add, op1=ALU.mult), reads=[mt, 'nrm_t', 'modc'], writes=['modc'])
        S.op('dve', lambda e: e.tensor_copy(out=modc[:, 32:40], in_=mod[:, 3, :]), reads=[mt, 'modc'], writes=['modc'])
        S.op('dve', lambda e: e.tensor_copy(out=modc[:, 40:48], in_=mod[:, 5, :]), reads=[mt, 'modc'], writes=['modc'])
        if stop == 1:
            S.dma('sp', x1T[0:128, 0:48], modc[:], reads=['modc'], key='st_dbg')
            S.wait_all('sp', ['st_dbg'])
            S.emit()
            return nc
        wp = C.sb("wp_bf", [128, 4, 2, 256], BF16)
        views = []
        for g in range(4):
            src = wpool[g].rearrange("(a p) o -> p a o", p=128)
            views.append((src, wp[:, g, :, :], (lambda qv: qv.rearrange("p (a b) -> p a b", a=2))))
        emit_cast_weight(C, wp, "wp_bf", views, stage)
        wup, wdn = load_ffn_weights(C, wup_d, wdn_d, stage)
        if stop == 2:
            S.dma('sp', x1T[0:128, 0:48], modc[:], reads=['modc', 'wup_bf', 'wdn_bf', 'wp_bf'], key='st_dbg')
            S.wait_all('sp', ['st_dbg'])
            S.emit()
            return nc
        xts = [(C.sb(f"xt{i}", [128, 8, W], F32), f"xt{i}") for i in range(2)]
        hf = C.sb("hf", [128, 8, W], F32)
        SA = C.sb("SA", [128, 2, W], F32)
        SB = C.sb("SB", [128, 2, W], F32)
        h2 = C.sb("h2", [128, 8, NT], BF16)
        a2 = C.sb("a2", [128, 32, NT], BF16)
        bufs = dict(xsq=C.sb("xsq", [128, 8, W], BF16), rstd=C.sb("rstd", [128, W], F32),
                    tmp=[(C.sb("tmpa", [128, W], F32), "tmpa"), (C.sb("tmpb", [128, W], F32), "tmpb")])
        rl = [(C.sb("rla", [128, 2 * NT], F32)[:], "rla"), (C.sb("rlb", [128, 2 * NT], F32)[:], "rlb")]
        fix = C.sb("fix", [128, 2, HALO], F32)
        pl = C.sb("pl", [128, 8, NT], BF16)

        def front_a_gen(t):
            xt, xtok = xts[t % 2]
            src = xTh[:, t, :].rearrange("(a p) w -> p a w", p=128)
            S.dma('sp', xt[:], src, writes=[xtok], key=xtok)
            yield from emit_norm_gen(C, xt, xtok, W, modc, modc, (0, 8), lambda c: (hf[:, c, :], 'hf'), bufs, 'n1')
            yield
            S.op('dve', (lambda e, t=t: e.tensor_scalar(out=hf[:, :, 0:HALO], in0=hf[:, :, 0:HALO], scalar1=hv[:, t:t + 1],
                                                       scalar2=None, op0=ALU.mult)), reads=['hf', 'hv'], writes=['hf'])
            yield
            for g in range(4):
                w = 2 << g
                hg = hf[:, 2 * g:2 * g + 2, :]
                cur, curtok = hg, 'hf'
                sh = 1
                k = 0
                while sh < w:
                    dst, dtok = (SA, 'SA') if k % 2 == 0 else (SB, 'SB')
                    lo = 2 * sh - 1
                    S.op('pool', (lambda e, dst=dst, cur=cur, sh=sh, lo=lo: e.tensor_tensor(
                        out=dst[:, :, lo:W], in0=cur[:, :, lo:W], in1=cur[:, :, lo - sh:W - sh], op=ALU.add)),
                        reads=[curtok], writes=[dtok])
                    cur, curtok = dst, dtok
                    sh *= 2
                    k += 1
                S.op('dve', (lambda e, g=g, cur=cur, w=w, hg=hg: e.scalar_tensor_tensor(
                    out=pl[:, 2 * g:2 * g + 2, :], in0=cur[:, :, HALO:W], scalar=1.0 / w, in1=hg[:, :, HALO:W],
                    op0=ALU.mult, op1=ALU.subtract)), reads=[curtok, 'hf'], writes=['pl'])
                if t == 0:
                    S.op('dve', (lambda e, g=g, cur=cur: e.tensor_tensor(
                        out=fix[:], in0=cur[:, :, HALO:2 * HALO], in1=ic[:, 2 * g:2 * g + 2, :], op=ALU.mult)),
                        reads=[curtok, 'ic'], writes=['fix'])
                    S.op('dve', (lambda e, g=g, hg=hg: e.tensor_tensor(
                        out=pl[:, 2 * g:2 * g + 2, 0:HALO], in0=fix[:], in1=hg[:, :, HALO:2 * HALO], op=ALU.subtract)),
                        reads=['fix', 'hf', 'pl'], writes=['pl'])
                yield
            yield
            for g in range(4):
                for oc in range(2):
                    ps, pst = C.ps(256)
                    for icn in range(2):
                        S.op('pe', (lambda e, ps=ps, g=g, oc=oc, icn=icn: e.matmul(
                            ps, lhsT=wp[:, g, icn, oc * 128:(oc + 1) * 128], rhs=pl[:, 2 * g + icn, :],
                            start=(icn == 0), stop=(icn == 1))), reads=['pl', 'wp_bf'], writes=[pst], track=(icn == 1))
                    dc = 2 * g + oc
                    S.op('dve', (lambda e, ps=ps, dc=dc, xt=xt: e.scalar_tensor_tensor(
                        out=xt[:, dc, HALO:W], in0=ps, scalar=modc[:, 16 + dc:17 + dc], in1=xt[:, dc, HALO:W],
                        op0=ALU.mult, op1=ALU.add)), reads=[pst, xtok, 'modc'], writes=[xtok])
            yield

        def front_b_gen(t):
            xt, xtok = xts[t % 2]
            yield from emit_norm_gen(C, xt[:, :, HALO:W], xtok, NT, modc, modc, (24, 32), lambda c: (h2[:, c, :], 'h2'), bufs, 'n2')

            yield

        def tile_ffn(t):
            xt, xtok = xts[t % 2]
            interleave(ffn_up_gen(C, wup, h2, a2, rl), front_a_gen(t + 1) if t + 1 < ntiles else None, every=2)
            interleave(ffn_down_gen(C, xt, xtok, HALO, wdn, a2, modc, 40), front_b_gen(t + 1) if t + 1 < ntiles else None)
            dst = x1T[:, t * NT:(t + 1) * NT].rearrange("(a p) n -> p a n", p=128)
            S.dma('sp', dst, xt[:, :, HALO:W], reads=[xtok], key=f'st_out{t % 2}')
        for _ in front_a_gen(0):
            pass
        for _ in front_b_gen(0):
            pass
        for t in range(ntiles):
            tile_ffn(t)
        S.wait_all('sp', [f'st_out{i}' for i in range(min(2, ntiles))])
        _end(fz, C)
    return nc


def build_p2(ntiles=NBLK, fz=None):
    with ExitStack() as es:
        nc, C, decl = _begin(fz, es, None, "p2_")
        x1T = decl("x1T", [D, TOK], F32, "ExternalInput")
        cT = decl("cT", [128, 8], F32, "ExternalInput")
        wmod = decl("wmod", [D, 6 * D], F32, "ExternalInput")
        bmod = decl("bmod", [128, 48], F32, "ExternalInput")
        nrm = decl("nrm", [128, 8], F32, "ExternalInput")
        wqkv = decl("wqkv", [D, 3 * D], F32, "ExternalInput")
        qT = decl("qT", [D, TOK], BF16, "ExternalOutput")
        kT = decl("kT", [D, TOK], BF16, "ExternalOutput")
        vO = decl("v", [TOK, D], BF16, "ExternalOutput")
        kmO = decl("kmean", [D, NBLK], F32, "ExternalOutput")
        S = C.S
        emit_consts(C)
        stage = [(C.sb("stg0", [128, 1024], F32), ["stg0a", "stg0b"]), (C.sb("stg1", [128, 1024], F32), ["stg1a", "stg1b"])]
        mod = C.sb("mod", [128, 2, 8], F32)
        nrm_t = C.sb("nrm_t", [128, 8], F32)
        S.dma('sp', nrm_t[:], nrm[:, :], writes=['nrm_t'], key='ld_nrm')
        emit_mod(C, cT, wmod, bmod, [0, 1], mod, stage)
        modc = C.sb("modc", [128, 16], F32)
        S.op('dve', lambda e: e.scalar_tensor_tensor(out=modc[:, 0:8], in0=mod[:, 1, :], scalar=1.0, in1=nrm_t[:, 0:8],
                                                     op0=ALU.add, op1=ALU.mult), reads=['mod', 'nrm_t'], writes=['modc'])
        S.op('dve', lambda e: e.tensor_copy(out=modc[:, 8:16], in_=mod[:, 0, :]), reads=['mod', 'modc'], writes=['modc'])
        wq = C.sb("wqkv_bf", [128, 8, 3 * D], BF16)
        views = []
        for dc in range(8):
            for q in range(3):
                src = wqkv[dc * 128:(dc + 1) * 128, q * 1024:(q + 1) * 1024]
                views.append((src, wq[:, dc, q * 1024:(q + 1) * 1024], '2d'))
        emit_cast_weight(C, wq, "wqkv_bf", views, stage)
        xts = [(C.sb(f"xt{i}", [128, 8, NT], F32), f"xt{i}") for i in range(2)]
        hs = [(C.sb(f"h2{i}", [128, 8, NT], BF16), f"h2{i}") for i in range(2)]
        qts = [(C.sb(f"qt{i}", [128, 8, NT], BF16), f"qt{i}") for i in range(2)]
        kts = [(C.sb(f"kt{i}", [128, 8, NT], BF16), f"kt{i}") for i in range(2)]
        vts = [(C.sb(f"vt{i}", [128, 2, D], BF16), f"vt{i}") for i in range(2)]
        km = C.sb("km", [128, 8, NBLK], F32)
        S.op('pool', lambda e: e.memset(km[:], 0.0), writes=['km'])
        bufs = dict(xsq=C.sb("xsq", [128, 8, NT], BF16), rstd=C.sb("rstd", [128, NT], F32),
                    tmp=[(C.sb("tmpa", [128, NT], F32), "tmpa"), (C.sb("tmpb", [128, NT], F32), "tmpb")])
        def norm_gen(t):
            xt, xtok = xts[t % 2]
            src = x1T[:, t * NT:(t + 1) * NT].rearrange("(a p) n -> p a n", p=128)
            S.dma('sp', xt[:], src, writes=[xtok], key=xtok)
            h, htok = hs[t % 2]
            yield from emit_norm_gen(C, xt, xtok, NT, modc, modc, (0, 8), lambda c: (h[:, c, :], htok), bufs, 'n1')
            yield

        def qkv_gen(t):
            h, htok = hs[t % 2]
            qt, qtok = qts[t % 2]
            kt, ktok = kts[t % 2]
            vt, vtok = vts[t % 2]
            for which in range(2):
                for o2 in range(4):
                    ps, pst = C.ps(512)
                    for k in range(2):
                        oc = 2 * o2 + k
                        col = which * D + oc * 128
                        for dc in range(8):
                            S.op('pe', (lambda e, ps=ps, dc=dc, col=col, k=k: e.matmul(
                                ps[:, k * NT:(k + 1) * NT], lhsT=wq[:, dc, col:col + 128], rhs=h[:, dc, :],
                                start=(dc == 0), stop=(dc == 7))),
                                reads=[htok, 'wqkv_bf'], writes=[pst], track=(dc == 7 and k == 1))
                    psv = ps.rearrange("p (a b) -> p a b", a=2)
                    if which == 0:
                        S.op('dve', (lambda e, psv=psv, o2=o2, qt=qt: e.tensor_scalar(
                            out=qt[:, 2 * o2:2 * o2 + 2, :], in0=psv, scalar1=0.125, scalar2=None, op0=ALU.mult)),
                            reads=[pst], writes=[qtok])
                    else:
                        for k in range(2):
                            oc = 2 * o2 + k
                            S.op('act', (lambda e, ps=ps, oc=oc, kt=kt, t=t, k=k: e.activation(
                                out=kt[:, oc, :], in_=ps[:, k * NT:(k + 1) * NT], func=AF.Copy, accum_out=km[:, oc, t:t + 1])),
                                reads=[pst, 'km'], writes=[ktok, 'km'])
                    yield
            for ts_ in range(2):
                for fh in range(2):
                    ps, pst = C.ps(512)
                    for dc in range(8):
                        S.op('pe', (lambda e, ps=ps, dc=dc, ts_=ts_, fh=fh: e.matmul(
                            ps, lhsT=h[:, dc, ts_ * 128:(ts_ + 1) * 128],
                            rhs=wq[:, dc, 2 * D + fh * 512:2 * D + (fh + 1) * 512],
                            start=(dc == 0), stop=(dc == 7))),
                            reads=[htok, 'wqkv_bf'], writes=[pst], track=(dc == 7))
                    S.op('dve', (lambda e, ps=ps, ts_=ts_, fh=fh, vt=vt: e.tensor_copy(
                        out=vt[:, ts_, fh * 512:(fh + 1) * 512], in_=ps)), reads=[pst], writes=[vtok])
                    yield
            S.dma('sp', qT[:, t * NT:(t + 1) * NT].rearrange("(a p) n -> p a n", p=128), qt[:], reads=[qtok], key=f'stq{t % 2}')
            if fz is None:
                S.dma('sp', kT[:, t * NT:(t + 1) * NT].rearrange("(a p) n -> p a n", p=128), kt[:], reads=[ktok], key=f'stk{t % 2}')
                S.dma('sp', vO[t * NT:(t + 1) * NT, :].rearrange("(a p) d -> p a d", p=128), vt[:], reads=[vtok], key=f'stv{t % 2}')
            else:
                for a in range(4):
                    S.dma('sp', fz.T["own_k"][a].ap()[:, t * NT:(t + 1) * NT].rearrange("(c p) n -> p c n", p=128),
                          kt[:, 2 * a:2 * a + 2, :], reads=[ktok], key=f'stk{t % 2}')
                S.dma('sp', fz.T["own_v"][t // 4].ap()[(t % 4) * NT:(t % 4 + 1) * NT, :].rearrange("(a p) d -> p a d", p=128),
                      vt[:], reads=[vtok], key=f'stv{t % 2}')
            yield

        for _ in norm_gen(0):
            pass
        for t in range(ntiles):
            interleave(qkv_gen(t), norm_gen(t + 1) if t + 1 < ntiles else None, every=2)
        S.op('dve', lambda e: e.tensor_scalar(out=km[:], in0=km[:], scalar1=1.0 / NT, scalar2=None, op0=ALU.mult),
             reads=['km'], writes=['km'])
        S.dma('sp', kmO[:, :].rearrange("(a p) n -> p a n", p=128), km[:], reads=['km'], key='stkm')
        nk = min(2, ntiles)
        S.wait_all('sp', [f'stq{i}' for i in range(nk)] + [f'stk{i}' for i in range(nk)] + [f'stv{i}' for i in range(nk)] + ['stkm'])
        _end(fz, C)
    return nc


NSLOT = 32
LF = 1792
LX = 1664


def build_p3(nheads=16, nblk=NBLK, fz=None):
    with ExitStack() as es:
        nc, C, decl = _begin(fz, es, {'S': [0, 1, 2, 3], 'acc': [4, 5], 'g': [6, 7]}, "p3_")
        qT = decl("qT", [D, TOK], BF16, "ExternalInput")
        kTs = decl("kTs", [D, NSLOT * NT], BF16, "ExternalInput")
        vS = decl("vS", [NSLOT * NT, D], BF16, "ExternalInput")
        kmS = decl("kmS", [D, NSLOT], F32, "ExternalInput")
        pm_d = decl("pm", [128, NBLK, NSLOT], F32, "ExternalInput")
        cfar_d = decl("cfar", [128, NBLK, NSLOT], F32, "ExternalInput")
        cown_d = decl("cown", [128, NBLK, NSLOT], F32, "ExternalInput")
        rb31_d = decl("rb31", [128, 16], F32, "ExternalInput")
        rbaug_d = decl("rbaug", [33, 16], F32, "ExternalInput")
        onehot_d = decl("onehot", [33, LF], F32, "ExternalInput")
        ind_d = decl("ind", [32, NSLOT * NT], BF16, "ExternalInput")
        jmat_d = decl("jmat", [128, 256], BF16, "ExternalInput")
        oT = decl("oT", [D, TOK], BF16, "ExternalOutput")
        Fd = nc.dram_tensor("Fd", [16, LF], BF16)
        S = C.S
        dyn = {}
        if False:
            KN = fz.T["KN"].ap()
            VN3t = fz.T["VN3"].ap().transpose([1, 0, 2])
            KMN = fz.T["KMN"].ap()
            hv_d = fz.T["hv"]

            def _setup(e):
                r1 = e.alloc_register("r_blk")
                r2 = e.alloc_register("r_tile")
                e.reg_load(r1, hv_d[0:1, 0:1])
                e.reg_load(r2, hv_d[0:1, 1:2])
                dyn['blk'] = e.snap(r1, min_val=0, max_val=1)
                dyn['tile'] = e.snap(r2, min_val=0, max_val=2)
            S.raw('sp', _setup)
        rbaug = C.sb("rbaug", [33, 16], F32)
        onehot = C.sb("onehot", [33, LF], F32)
        Fsb = C.sb("Fsb", [16, LF], BF16)
        S.dma('sp', rbaug[:], rbaug_d[:, :], writes=['rbaug'], key='ld_rbaug')
        S.dma('sp', onehot[:], onehot_d[:, :], writes=['onehot'], key='ld_onehot')
        for k in range(4):
            ps, pst = C.ps(512, 'g')
            wd = min(512, LF - k * 512)
            S.op('pe', (lambda e, ps=ps, k=k, wd=wd: e.matmul(ps[0:16, 0:wd], lhsT=rbaug[:], rhs=onehot[:, k * 512:k * 512 + wd],
                                                             start=True, stop=True)), reads=['rbaug', 'onehot'], writes=[pst])
            S.op('dve', (lambda e, ps=ps, k=k, wd=wd: e.tensor_copy(out=Fsb[:, k * 512:k * 512 + wd], in_=ps[0:16, 0:wd])),
                 reads=[pst], writes=['Fsb'])
        S.dma('sp', Fd.ap()[:, :], Fsb[:], reads=['Fsb'], writes=['Fd'], key='st_Fd')
        pm = C.sb("pm", [128, NBLK, NSLOT], F32)
        cfar = C.sb("cfar", [128, NBLK, NSLOT], F32)
        cown = C.sb("cown", [128, NBLK, NSLOT], F32)
        rb31 = C.sb("rb31", [128, 16], F32)
        jmat = C.sb("jmat", [128, 256], BF16)
        for nm, t_, d_ in (('pm', pm, pm_d), ('cfar', cfar, cfar_d), ('cown', cown, cown_d)):
            S.dma('sp', t_[:], d_[:, :, :], writes=[nm], key='ld_' + nm)
        S.dma('sp', rb31[:], rb31_d[:, :], writes=['rb31'], key='ld_rb31')
        S.dma('sp', jmat[:], jmat_d[:, :], writes=['jmat'], key='ld_jmat')
        J = jmat[:, 0:128]
        Iden = jmat[:, 128:256]
        kas = [(C.sb(f"ka{i}", [96, NSLOT * NT], BF16), f"ka{i}") for i in range(2)]
        qas = [(C.sb(f"qa{i}", [96, TOK], BF16), f"qa{i}") for i in range(2)]
        vas = [(C.sb(f"va{i}", [128, 2 * NSLOT, 128], BF16), f"va{i}") for i in range(2)]
        Bhs = [(C.sb(f"Bh{i}", [128, LX], BF16), f"Bh{i}") for i in range(2)]
        kmfs = [(C.sb(f"kmf{i}", [64, NSLOT], F32), f"kmf{i}") for i in range(2)]
        kmbs = [(C.sb(f"kmb{i}", [64, NSLOT], BF16), f"kmb{i}") for i in range(2)]
        ohs = [(C.sb(f"oh{i}", [64, TOK], BF16), f"oh{i}") for i in range(2)]
        Chs = [(C.sb(f"Ch{i}", [128, NBLK, NSLOT], F32), f"Ch{i}") for i in range(2)]
        vps = [(C.sb(f"vp{i}", [128, 96], BF16), f"vp{i}") for i in range(2)]
        pTs = [(C.sb(f"pT{i}", [128, 512], BF16), f"pT{i}") for i in range(3)]
        gms = [(C.sb(f"gm{i}", [128, NSLOT], F32), f"gm{i}") for i in range(2)]
        mxs = [(C.sb(f"mx{i}", [128, 8], F32), f"mx{i}") for i in range(2)]
        As = [(C.sb(f"A{i}", [128, NSLOT], F32), f"A{i}") for i in range(2)]
        rds = [(C.sb(f"rd{i}", [128, NT], F32), f"rd{i}") for i in range(2)]
        rd0s = [(C.sb(f"rdz{i}", [64, NT], F32), f"rdz{i}") for i in range(2)]
        for i in range(2):
            S.dma('sp', kas[i][0][64:96, :], ind_d[:, :], writes=[kas[i][1] + 'i'], key=f'ld_ind{i}')
            S.op('pool', (lambda e, i=i: e.memset(vas[i][0][:, :, 64:128], 1.0)), writes=[vas[i][1] + 'o'])
            S.op('pool', (lambda e, i=i: e.memset(vps[i][0][:], 0.0)), writes=[vps[i][1]])
        pTs = pTs + [(C.sb("pT3", [128, 512], BF16), "pT3")]
        state = dict(cnt=0, pcnt=0)

        def bufs_of(h):
            return dict(ka=kas[h % 2], qa=qas[h % 2], va=vas[h % 2], Bh=Bhs[h % 2], kmf=kmfs[h % 2], kmb=kmbs[h % 2],
                        oh=ohs[h % 2], Ch=Chs[h % 2])

        def emit_loads(h):
            B = bufs_of(h)
            ka, katok = B['ka']
            qa, qatok = B['qa']
            va, vatok = B['va']
            Bh, Bhtok = B['Bh']
            kmf, kmftok = B['kmf']
            S.dma('sp', kmf[:], kmS[h * 64:(h + 1) * 64, :], writes=[kmftok], key=kmftok)
            S.dma('sp', qa[0:64, :], qT[h * 64:(h + 1) * 64, :], writes=[qatok], key=qatok)
            S.dma('sp', Bh[:], bass.AP(Fd, h * LF, [[1, 128], [1, LX]]), reads=['Fd'], writes=[Bhtok], key=Bhtok)
            if fz is None:
                S.dma('sp', ka[0:64, :], kTs[h * 64:(h + 1) * 64, :], writes=[katok], key=katok)
                for pc in range(4):
                    src = vS[pc * 2048:(pc + 1) * 2048, h * 64:(h + 1) * 64].rearrange("(a p) d -> p a d", p=128)
                    S.dma('sp', va[:, pc * 16:(pc + 1) * 16, 0:64], src, writes=[vatok], key=vatok)
            else:
                rl = (h % 4) * 64
                gk = fz.T['G_k'][h // 4].ap().rearrange("(e r) (m t) -> r m e t", e=2, t=NT)[rl:rl + 64]
                S.dma('sp', ka[0:64, :].rearrange("p (m e t) -> p m e t", m=16, e=2), gk, writes=[katok], key=katok)
                for pc in range(4):
                    for e_ in range(2):
                        for m_ in range(4):
                            k0 = pc * 16 + m_ * 4 + 2 * e_
                            r0 = e_ * 1024 + m_ * NT
                            src = fz.T['G_v'][pc].ap()[r0:r0 + NT, h * 64:(h + 1) * 64].rearrange("(u p) d -> p u d", p=128)
                            S.dma('sp', va[:, k0:k0 + 2, 0:64], src, writes=[vatok], key=vatok)

        def gate_gen(h):
            B = bufs_of(h)
            qa, qatok = B['qa']
            kmf, kmftok = B['kmf']
            kmb, kmbtok = B['kmb']
            Ch, Chtok = B['Ch']
            qam = qatok + 'm'
            S.op('dve', (lambda e: e.tensor_copy(out=kmb[:], in_=kmf[:])), reads=[kmftok], writes=[kmbtok])
            S.op('dve', (lambda e: e.scalar_tensor_tensor(out=Ch[:], in0=cfar[:], scalar=rb31[:, h:h + 1], in1=cown[:],
                                                          op0=ALU.mult, op1=ALU.add)),
                 reads=['cfar', 'cown', 'rb31'], writes=[Chtok])
            for i in range(nblk):
                for ch in range(2):
                    q0 = i * NT + ch * 128
                    cnt = state['cnt']
                    gm, gmtok = gms[cnt % 2]
                    mx, mxtok = mxs[cnt % 2]
                    A, Atok = As[cnt % 2]
                    vp, vptok = vps[cnt % 2]
                    state['cnt'] += 1
                    ps, pst = C.ps(512, 'g')
                    S.op('pe', (lambda e, ps=ps, q0=q0: e.matmul(
                        ps[:, 0:NSLOT], lhsT=qa[0:64, q0:q0 + 128], rhs=kmb[:], start=True, stop=True)),
                        reads=[qatok, kmbtok], writes=[pst])
                    S.op('dve', (lambda e, ps=ps, gm=gm, i=i: e.tensor_tensor(out=gm[:], in0=ps[:, 0:NSLOT], in1=pm[:, i, :], op=ALU.add)),
                         reads=[pst, 'pm'], writes=[gmtok])
                    S.op('dve', (lambda e, gm=gm, mx=mx: e.max(out=mx[:], in_=gm[:])), reads=[gmtok], writes=[mxtok])
                    S.op('dve', (lambda e, gm=gm, mx=mx, A=A: e.tensor_scalar(out=A[:], in0=gm[:], scalar1=mx[:, 2:3], scalar2=-1.0,
                                                                                op0=ALU.is_ge, op1=ALU.add)),
                         reads=[gmtok, mxtok], writes=[Atok])
                    S.op('dve', (lambda e, A=A, vp=vp, i=i: e.scalar_tensor_tensor(
                        out=vp[:, 64:96], in0=A[:], scalar=BIG, in1=Ch[:, i, :], op0=ALU.mult, op1=ALU.add)),
                        reads=[Atok, Chtok], writes=[vptok])
                    yield
                    ps2, pst2 = C.ps(512, 'g')
                    S.op('pe', (lambda e, ps2=ps2, vp=vp: e.matmul(ps2[0:96, 0:128], lhsT=vp[:], rhs=Iden, start=True, stop=True)),
                         reads=[vptok, 'jmat'], writes=[pst2])
                    S.op('dve', (lambda e, ps2=ps2, q0=q0: e.tensor_copy(out=qa[64:96, q0:q0 + 128], in_=ps2[64:96, 0:128])),
                         reads=[pst2], writes=[qam])
                    yield

        def attn_gen(h):
            B = bufs_of(h)
            ka, katok = B['ka']
            qa, qatok = B['qa']
            va, vatok = B['va']
            Bh, Bhtok = B['Bh']
            oh, ohtok = B['oh']
            qam = qatok + 'm'
            items = [(i, s) for i in range(nblk) for s in range(2 * i + 2)]
            N = len(items)
            LAG = 3
            info = {}
            blk = {}

            def emit_qk(n):
                i, s = items[n]
                own = 2 * i + 1
                near = (own - s) <= 5
                sps, spst = C.ps(512, 'S')
                for k2 in range(2):
                    kt = 2 * s + k2
                    S.op('pe', (lambda e, sps=sps, kt=kt, k2=k2, i=i, near=near: e.matmul(
                        sps[:, k2 * NT:(k2 + 1) * NT], lhsT=ka[0:96, kt * 128:(kt + 1) * 128], rhs=qa[0:96, i * NT:(i + 1) * NT],
                        start=True, stop=(not near))),
                        reads=[katok, katok + 'i', qatok, qam], writes=[spst], track=(not near and k2 == 1))
                    if near:
                        x0 = (own - s) * NT - k2 * 128 + 128
                        S.op('pe', (lambda e, sps=sps, x0=x0, k2=k2: e.matmul(
                            sps[:, k2 * NT:(k2 + 1) * NT], lhsT=J, rhs=Bh[:, x0:x0 + NT], start=False, stop=True)),
                            reads=[Bhtok, 'jmat'], writes=[spst], track=(k2 == 1))
                pT, pTtok = pTs[state['pcnt'] % 4]
                state['pcnt'] += 1
                S.op('act', (lambda e, sps=sps, pT=pT: e.activation(out=pT[:], in_=sps, func=AF.Exp)), reads=[spst], writes=[pTtok])
                info[n] = (pT, pTtok)

            def emit_pv(n):
                i, s = items[n]
                own = 2 * i + 1
                pT, pTtok = info.pop(n)
                if s == 0:
                    blk[i] = C.ps(512, 'acc')
                acc, acctok = blk[i]
                nmm = 2 * (own + 1)
                for k2 in range(2):
                    kt = 2 * s + k2
                    mi = 2 * s + k2
                    S.op('pe', (lambda e, acc=acc, pT=pT, kt=kt, k2=k2, mi=mi, nmm=nmm: e.matmul(
                        acc[:, 0:NT], lhsT=va[:, kt, :], rhs=pT[:, k2 * NT:(k2 + 1) * NT], start=(mi == 0), stop=(mi == nmm - 1))),
                        reads=[vatok, vatok + 'o', pTtok], writes=[acctok], track=(k2 == 1))
                if s == own:
                    rd, rdtok = rds[i % 2]
                    rd0, rd0tok = rd0s[i % 2]
                    S.op('dve', (lambda e, acc=acc, rd=rd: e.reciprocal(out=rd[64:128, :], in_=acc[64:128, 0:NT])), reads=[acctok], writes=[rdtok])
                    S.op('dve', (lambda e, rd=rd, rd0=rd0: e.tensor_copy(out=rd0[0:64, :], in_=rd[64:128, :])), reads=[rdtok], writes=[rd0tok])
                    S.op('dve', (lambda e, acc=acc, rd0=rd0, i=i: e.tensor_tensor(out=oh[0:64, i * NT:(i + 1) * NT], in0=acc[0:64, 0:NT],
                                                                               in1=rd0[0:64, :], op=ALU.mult)),
                         reads=[acctok, rd0tok], writes=[ohtok])
                    del blk[i]

            for n in range(N + LAG):
                if n < N:
                    emit_qk(n)
                if n >= LAG:
                    emit_pv(n - LAG)
                yield
            S.dma('sp', oT[h * 64:(h + 1) * 64, 0:nblk * NT], oh[0:64, 0:nblk * NT], reads=[ohtok], key=f'st_o{h % 2}')

        GATE_START = 64 if nblk == NBLK else 2
        emit_loads(0)
        for _ in gate_gen(0):
            pass
        for h in range(nheads):
            gg = None
            if h + 1 < nheads:
                emit_loads(h + 1)
                gg = gate_gen(h + 1)
            step = 0
            for _ in attn_gen(h):
                step += 1
                if gg is not None and step >= GATE_START and step % 3 == 0:
                    if next(gg, 'done') == 'done':
                        gg = None
            if gg is not None:
                for _ in gg:
                    pass
        _end(fz, C)
    return nc


def build_p4(ntiles=NBLK, fz=None):
    with ExitStack() as es:
        nc, C, decl = _begin(fz, es, None, "p4_")
        x1T = decl("x1T", [D, TOK], F32, "ExternalInput")
        oT = decl("oT", [D, TOK], BF16, "ExternalInput")
        cT = decl("cT", [128, 8], F32, "ExternalInput")
        wmod = decl("wmod", [D, 6 * D], F32, "ExternalInput")
        bmod = decl("bmod", [128, 48], F32, "ExternalInput")
        nrm = decl("nrm", [128, 16], F32, "ExternalInput")
        wo_d = decl("wo", [D, D], F32, "ExternalInput")
        wup_d = decl("wup", [D, DFF], F32, "ExternalInput")
        wdn_d = decl("wdn", [DFF, D], F32, "ExternalInput")
        outT = decl("outT", [D, TOK], F32, "ExternalOutput")
        S = C.S
        emit_consts(C)
        stage = [(C.sb("stg0", [128, 1024], F32), ["stg0a", "stg0b"]), (C.sb("stg1", [128, 1024], F32), ["stg1a", "stg1b"])]
        mod = C.sb("mod", [128, 4, 8], F32)
        nrm_t = C.sb("nrm_t", [128, 16], F32)
        S.dma('sp', nrm_t[:], nrm[:, :], writes=['nrm_t'], key='ld_nrm')
        emit_mod(C, cT, wmod, bmod, [2, 3, 4, 5], mod, stage)
        modc = C.sb("modc", [128, 48], F32)
        S.op('pool', lambda e: e.memset(modc[:], 0.0), writes=['modc'])
        S.op('dve', lambda e: e.tensor_copy(out=modc[:, 0:8], in_=mod[:, 0, :]), reads=['mod', 'modc'], writes=['modc'])
        S.op('dve', lambda e: e.scalar_tensor_tensor(out=modc[:, 8:16], in0=mod[:, 2, :], scalar=1.0, in1=nrm_t[:, 0:8],
                                                     op0=ALU.add, op1=ALU.mult), reads=['mod', 'nrm_t', 'modc'], writes=['modc'])
        S.op('dve', lambda e: e.tensor_copy(out=modc[:, 16:24], in_=mod[:, 1, :]), reads=['mod', 'modc'], writes=['modc'])
        S.op('dve', lambda e: e.tensor_copy(out=modc[:, 24:32], in_=mod[:, 3, :]), reads=['mod', 'modc'], writes=['modc'])
        S.op('dve', lambda e: e.tensor_copy(out=modc[:, 32:40], in_=nrm_t[:, 8:16]), reads=['nrm_t', 'modc'], writes=['modc'])
        wo = C.sb("wo_bf", [128, 8, D], BF16)
        views = []
        for fc in range(8):
            src = wo_d[fc * 128:(fc + 1) * 128, :]
            views.append((src, wo[:, fc, :], '2d'))
        emit_cast_weight(C, wo, "wo_bf", views, stage)
        wup, wdn = load_ffn_weights(C, wup_d, wdn_d, stage)
        xts = [(C.sb(f"xt{i}", [128, 8, NT], F32), f"xt{i}") for i in range(2)]
        ots = [(C.sb(f"ot{i}", [128, 8, NT], BF16), f"ot{i}") for i in range(2)]
        h2 = C.sb("h2", [128, 8, NT], BF16)
        a2 = C.sb("a2", [128, 32, NT], BF16)
        bufs = dict(xsq=C.sb("xsq", [128, 8, NT], BF16), rstd=C.sb("rstd", [128, NT], F32),
                    tmp=[(C.sb("tmpa", [128, NT], F32), "tmpa"), (C.sb("tmpb", [128, NT], F32), "tmpb")])
        rl = [(C.sb("rla", [128, 2 * NT], F32)[:], "rla"), (C.sb("rlb", [128, 2 * NT], F32)[:], "rlb")]

        def front_gen(t):
            xt, xtok = xts[t % 2]
            ot, ottok = ots[t % 2]
            S.dma('sp', xt[:], x1T[:, t * NT:(t + 1) * NT].rearrange("(a p) n -> p a n", p=128), writes=[xtok], key=xtok)
            S.dma('act', ot[:], oT[:, t * NT:(t + 1) * NT].rearrange("(a p) n -> p a n", p=128), writes=[ottok], key=ottok)
            for d2 in range(4):
                ps, pst = C.ps(512)
                for k in range(2):
                    dc = 2 * d2 + k
                    for fc in range(8):
                        S.op('pe', (lambda e, ps=ps, dc=dc, fc=fc, k=k, ot=ot: e.matmul(
                            ps[:, k * NT:(k + 1) * NT], lhsT=wo[:, fc, dc * 128:(dc + 1) * 128], rhs=ot[:, fc, :],
                            start=(fc == 0), stop=(fc == 7))), reads=[ottok, 'wo_bf'], writes=[pst], track=(fc == 7 and k == 1))
                for k in range(2):
                    dc = 2 * d2 + k
                    S.op('dve', (lambda e, ps=ps, dc=dc, k=k, xt=xt: e.scalar_tensor_tensor(
                        out=xt[:, dc, :], in0=ps[:, k * NT:(k + 1) * NT], scalar=modc[:, dc:dc + 1], in1=xt[:, dc, :],
                        op0=ALU.mult, op1=ALU.add)), reads=[pst, xtok, 'modc'], writes=[xtok])
                yield
            emit_norm(C, xt, xtok, NT, modc, modc, (8, 16), lambda c: (h2[:, c, :], 'h2'), bufs, 'n2')
            yield

        def back_gen(t):
            xt, xtok = xts[t % 2]
            emit_norm(C, xt, xtok, NT, modc, modc, (32, 40), lambda c: (xt[:, c, :], xtok), bufs, 'nf')
            S.dma('sp', outT[:, t * NT:(t + 1) * NT].rearrange("(a p) n -> p a n", p=128), xt[:], reads=[xtok], key=f'st_out{t % 2}')
            yield

        for _ in front_gen(0):
            pass
        for t in range(ntiles):
            xt, xtok = xts[t % 2]
            interleave(ffn_up_gen(C, wup, h2, a2, rl), back_gen(t - 1) if t > 0 else None, every=3)
            interleave(ffn_down_gen(C, xt, xtok, 0, wdn, a2, modc, 24), front_gen(t + 1) if t + 1 < ntiles else None)
        for _ in back_gen(ntiles - 1):
            pass
        _end(fz, C)
    return nc


def build_px(fz):
    with ExitStack() as es:
        nc, C, decl = _begin(fz, es, None, "px_")
        S = C.S
        T = fz.T
        groups = [[0, 1], [2, 3], [4, 5], [6, 7]]
        pairs = [(T['own_k'][a], T['G_k'][a], f'k{a}') for a in range(4)] + [(T['own_v'][a], T['G_v'][a], f'v{a}') for a in range(4)]
        pairs.append((T['own_km'], T['G_km'], 'km'))
        for own, G, nm in pairs:
            S.custom('pool', (lambda e, own=own, G=G: e.collective_compute(
                "AllGather", ALU.bypass, replica_groups=groups, ins=[own.ap().opt()], outs=[G.ap().opt()])),
                writes=['G_' + nm], key='cc_' + nm)
        if os.environ.get('PX_STOP') == '1':
            _end(fz, C)
            return
        KMN = T['KMN'].ap()
        Gkm = T['G_km'].ap()
        gk = C.sb("gkm", [128, 2, 8, 16], F32)
        kmn = C.sb("kmn", [128, 8, NSLOT], F32)
        S.dma('sp', gk[:], Gkm.rearrange("(e a p) m -> p e a m", e=2, a=8), reads=['G_km'], writes=['gkm'], key='ld_gkm')
        S.op('pool', lambda e: e.memset(kmn[:], 0.0), writes=['kmn'])
        for e_ in range(2):
            S.op('dve', (lambda e, e_=e_: e.tensor_copy(
                out=kmn[:].rearrange("p a (m e) -> p a m e", e=2)[:, :, :, e_], in_=gk[:, e_, :, :])),
                reads=['gkm', 'kmn'], writes=['kmn'])
        S.dma('sp', KMN.rearrange("(a p) s -> p a s", p=128), kmn[:], reads=['kmn'], writes=['KMN'], key='st_kmn')
        _end(fz, C)


def build_fused():
    nc = bass.Bass("TRN2", target_bir_lowering=False)
    I32 = mybir.dt.int32
    with ExitStack() as ges:
        C = Ctx(nc, ges)
        T = {}

        def ext(name, shape, dt, kind="ExternalInput"):
            return nc.dram_tensor(name, shape, dt, kind=kind).ap()
        T["p1_xTh"] = ext("xTh", [D, NBLK, HALO + NT], F32)
        cT = ext("cT", [128, 8], F32)
        wmod0 = ext("wmod0", [D, 6 * D], F32)
        wmod1 = ext("wmod1", [D, 6 * D], F32)
        bmod0 = ext("bmod0", [128, 48], F32)
        bmod1 = ext("bmod1", [128, 48], F32)
        T["p1_nrm"] = ext("nrm1", [128, 24], F32)
        T["p2_nrm"] = ext("nrm2", [128, 8], F32)
        T["p4_nrm"] = ext("nrm4", [128, 16], F32)
        T["p1_hval"] = ext("hval", [128, NBLK], F32)
        T["p1_invc"] = ext("invc", [128, 8, HALO], F32)
        T["p1_wpool"] = ext("wpool", [4, 256, 256], F32)
        T["p1_wup"] = ext("wup0", [D, DFF], F32)
        T["p1_wdn"] = ext("wdn0", [DFF, D], F32)
        T["p2_wqkv"] = ext("wqkv", [D, 3 * D], F32)
        T["p4_wo"] = ext("wo", [D, D], F32)
        T["p4_wup"] = ext("wup1", [D, DFF], F32)
        T["p4_wdn"] = ext("wdn1", [DFF, D], F32)
        T["p3_pm"] = ext("pm", [128, NBLK, NSLOT], F32)
        T["p3_cfar"] = ext("cfar", [128, NBLK, NSLOT], F32)
        T["p3_cown"] = ext("cown", [128, NBLK, NSLOT], F32)
        T["p3_rb31"] = ext("rb31", [128, 16], F32)
        T["p3_rbaug"] = ext("rbaug", [33, 16], F32)
        T["p3_onehot"] = ext("onehot", [33, LF], F32)
        T["p3_ind"] = ext("ind", [32, NSLOT * NT], BF16)
        T["p3_jmat"] = ext("jmat", [128, 256], BF16)
        for p in ("p1_", "p2_", "p4_"):
            T[p + "cT"] = cT
        T["p1_wmod"], T["p1_bmod"] = wmod0, bmod0
        T["p2_wmod"], T["p2_bmod"] = wmod1, bmod1
        T["p4_wmod"], T["p4_bmod"] = wmod1, bmod1
        x1T = nc.dram_tensor("x1T", [D, TOK], F32)
        qT = nc.dram_tensor("qT", [D, TOK], BF16)
        T["own_k"] = [nc.dram_tensor(f"own_k{a}", [256, TOK], BF16) for a in range(4)]
        T["own_v"] = [nc.dram_tensor(f"own_v{a}", [1024, D], BF16) for a in range(4)]
        T["own_km"] = nc.dram_tensor("own_km", [D, NBLK], F32)
        T["G_k"] = [nc.dram_tensor(f"G_k{a}", [512, TOK], BF16) for a in range(4)]
        T["G_v"] = [nc.dram_tensor(f"G_v{a}", [2048, D], BF16) for a in range(4)]
        T["G_km"] = nc.dram_tensor("G_km", [2 * D, NBLK], F32)
        T["KN"] = nc.dram_tensor("KN", [D, NSLOT, NT], BF16)
        T["VN3"] = nc.dram_tensor("VN3", [2 * NSLOT, 128, D], BF16)
        T["KMN"] = nc.dram_tensor("KMN", [D, NSLOT], F32)
        T["p3_kTs"] = T["KN"].ap().rearrange("f s t -> f (s t)")
        T["p3_vS"] = T["VN3"].ap().rearrange("a p d -> (a p) d")
        T["p3_kmS"] = T["KMN"].ap()
        oT = nc.dram_tensor("oT", [D, TOK], BF16)
        T["p1_x1T"] = T["p2_x1T"] = T["p4_x1T"] = x1T.ap()
        T["p2_qT"] = T["p3_qT"] = qT.ap()
        T["p2_kmean"] = T["own_km"].ap()
        T["p3_oT"] = T["p4_oT"] = oT.ap()
        T["p4_outT"] = ext("outT", [D, TOK], F32, kind="ExternalOutput")
        fz = Fz(nc, C, T)
        build_p1(fz=fz)
        build_p2(fz=fz)
        build_px(fz)
        build_p3(fz=fz)
        build_p4(fz=fz)
    return nc


def core_blocks(c):
    b, half = c // 2, c % 2
    return b, half, [2 * i + half for i in range(NBLK)]

def fm(v):
    v = np.asarray(v, np.float32)
    return np.ascontiguousarray(v.reshape(-1, 128).T)

def p1_inputs(c, x, cvec, w_mod, b_mod, norm_mix, norm_mlp, pool_scale, w_pool, w_up, w_down, layer=0):
    b, half, blks = core_blocks(c)
    xb = x[b]
    xTh = np.zeros((D, NBLK, HALO + NT), np.float32)
    for i, j in enumerate(blks):
        s = j * NT
        if j > 0:
            xTh[:, i, :HALO] = xb[s - HALO:s].T
        xTh[:, i, HALO:] = xb[s:s + NT].T
    hval = np.ones((128, NBLK), np.float32)
    if half == 0:
        hval[:, 0] = 0.0
    invc = np.zeros((128, 8, HALO), np.float32)
    for g, w in enumerate((2, 4, 8, 16)):
        for t in range(HALO):
            cnt = min(t + 1, w) if half == 0 else w
            invc[:, 2 * g:2 * g + 2, t] = 1.0 / cnt
    nrm = np.concatenate([fm(norm_mix[layer]), fm(norm_mlp[layer]), fm(pool_scale[0])], axis=1)
    return dict(xTh=xTh, cT=fm(cvec[b]), wmod=np.ascontiguousarray(w_mod[layer]), bmod=fm(b_mod[layer]), nrm=nrm,
                hval=hval, invc=invc, wpool=np.ascontiguousarray(w_pool[0]), wup=np.ascontiguousarray(w_up[layer]),
                wdn=np.ascontiguousarray(w_down[layer]))

NSLOT = 32; LF = 1792; BIG = 30000.0
import ml_dtypes
BF = ml_dtypes.bfloat16

def t5_onehot(half):
    import jax, jax.numpy as jnp, math
    d = np.arange(LF) - 255 - 256 * (1 - half)
    with jax.default_device(jax.devices('cpu')[0]):
        dist = jnp.maximum(jnp.asarray(d, jnp.int32), 0)
        nf = jnp.maximum(dist, 1).astype(jnp.float32)
        large = 16 + (jnp.log(nf / 16) / math.log(1024 / 16) * 16).astype(jnp.int32)
        large = jnp.minimum(large, 31)
        bucket = np.asarray(jnp.where(dist < 16, dist, large))
    oh = np.zeros((33, LF), np.float32)
    for y in range(LF):
        if d[y] >= 0:
            oh[bucket[y], y] = 1.0
        else:
            oh[32, y] = -BIG
    return oh

def p3_consts(half):
    pm = np.zeros((NBLK, NSLOT), np.float32); cfar = np.zeros_like(pm); cown = np.zeros_like(pm)
    for i in range(NBLK):
        own_u = 2 * i + 1
        own_t = 2 * i + half
        for s in range(NSLOT):
            if s == own_t:
                pm[i, s] = -2 * BIG; cown[i, s] = BIG
            elif s > own_t:
                pm[i, s] = -BIG
                if s <= own_u: cown[i, s] = -BIG
            else:
                if own_u - s >= 6: cfar[i, s] = 1.0
    rep = lambda a: np.ascontiguousarray(np.broadcast_to(a[None], (128,) + a.shape))
    ind = np.zeros((32, NSLOT * NT), np.float32)
    for s in range(NSLOT): ind[s, s * NT:(s + 1) * NT] = 1.0
    jm = np.zeros((128, 256), np.float32)
    jm[np.arange(128), 127 - np.arange(128)] = 1.0
    jm[np.arange(128), 128 + np.arange(128)] = 1.0
    return dict(pm=rep(pm), cfar=rep(cfar), cown=rep(cown), ind=ind.astype(BF), jmat=jm.astype(BF))

def p3_inputs(c, qT_own, kT_pair, v_pair, km_pair, rel_bias, consts, onehots):
    b, half, blks = core_blocks(c)
    kTs = np.zeros((D, NSLOT * NT), BF); vS = np.zeros((NSLOT * NT, D), BF); kmS = np.zeros((D, NSLOT), np.float32)
    for j in range(32):
        r, l = j % 2, j // 2
        kTs[:, j * NT:(j + 1) * NT] = kT_pair[r][:, l * NT:(l + 1) * NT]
        vS[j * NT:(j + 1) * NT] = v_pair[r][l * NT:(l + 1) * NT]
        kmS[:, j] = km_pair[r][:, l]
    rbaug = np.concatenate([rel_bias, np.ones((1, 16), np.float32)], 0).astype(np.float32)
    rb31 = np.ascontiguousarray(np.broadcast_to(rel_bias[31][None], (128, 16))).astype(np.float32)
    d = dict(qT=qT_own, kTs=kTs, vS=vS, kmS=kmS, rb31=rb31, rbaug=rbaug, onehot=onehots[half])
    d.update(consts[half])
    return d


def kernel(x, c, rel_bias, w_mod, b_mod, norm_mix, norm_mlp, w_pool, pool_scale, w_qkv, w_o, w_up, w_down, norm_final):
    f32 = lambda a: np.ascontiguousarray(np.asarray(a, dtype=np.float32))
    x, c, rel_bias, w_mod, b_mod = f32(x), f32(c), f32(rel_bias), f32(w_mod), f32(b_mod)
    norm_mix, norm_mlp, w_pool, pool_scale = f32(norm_mix), f32(norm_mlp), f32(w_pool), f32(pool_scale)
    w_qkv, w_o, w_up, w_down, norm_final = f32(w_qkv), f32(w_o), f32(w_up), f32(w_down), f32(norm_final)
    NC = 8
    cores = list(range(NC))
    nc = build_fused()
    consts = [p3_consts(0), p3_consts(1)]
    onehots = [t5_onehot(0), t5_onehot(1)]
    rbaug = np.concatenate([rel_bias, np.ones((1, 16), np.float32)], 0).astype(np.float32)
    rb31 = np.ascontiguousarray(np.broadcast_to(rel_bias[31][None], (128, 16))).astype(np.float32)
    shared = dict(wmod0=np.ascontiguousarray(w_mod[0]), wmod1=np.ascontiguousarray(w_mod[1]), bmod0=fm(b_mod[0]), bmod1=fm(b_mod[1]),
                  nrm2=fm(norm_mix[1]), nrm4=np.concatenate([fm(norm_mlp[1]), fm(norm_final)], axis=1),
                  wpool=np.ascontiguousarray(w_pool[0]), wup0=np.ascontiguousarray(w_up[0]), wdn0=np.ascontiguousarray(w_down[0]),
                  wqkv=np.ascontiguousarray(w_qkv[0]), wo=np.ascontiguousarray(w_o[0]), wup1=np.ascontiguousarray(w_up[1]),
                  wdn1=np.ascontiguousarray(w_down[1]), rb31=rb31, rbaug=rbaug)
    in_maps = []
    for cc in cores:
        b, half, blks = core_blocks(cc)
        p1 = p1_inputs(cc, x, c, w_mod, b_mod, norm_mix, norm_mlp, pool_scale, w_pool, w_up, w_down)
        d = dict(shared)
        d.update(xTh=p1["xTh"], cT=p1["cT"], nrm1=p1["nrm"], hval=p1["hval"], invc=p1["invc"], onehot=onehots[half])
        d.update(consts[half])
        in_maps.append(d)
    res = run_bass_kernel_spmd(nc, in_maps, core_ids=cores).results
    out = np.zeros(x.shape, np.float32)
    for cc in cores:
        b, half, blks = core_blocks(cc)
        oTc = np.asarray(res[cc]["outT"])
        for i, j in enumerate(blks):
            out[b, j * NT:(j + 1) * NT, :] = oTc[:, i * NT:(i + 1) * NT].T
    return out
```

```python
import os
import numpy as np
from contextlib import ExitStack
import concourse.bass as bass
import concourse.mybir as mybir
from concourse.bass_utils import run_bass_kernel_spmd


ENGS = ['pe', 'act', 'dve', 'pool', 'sp']


class Sched:
    def __init__(self, nc, es, same_sync=True):
        self.nc = nc
        self.es = es
        self.same_sync = same_sync
        self.ops = {e: [] for e in ENGS}
        self.sem = {e: es.enter_context(nc.semaphore("s_" + e)) for e in ENGS}
        self.cnt = {e: 0 for e in ENGS}
        self.seen = {e: {} for e in ENGS}
        self.lastw = {}
        self.readers = {}
        self.dsem = {}
        self.dcnt = {}

    def _handle(self, k):
        return self.sem[k] if k in self.sem else self.dsem[k]

    def _deps(self, eng, reads, writes):
        deps = {}

        def add(ev):
            k, v = ev
            if k == eng and (eng == 'pe' or not self.same_sync):
                return
            if deps.get(k, 0) < v:
                deps[k] = v
        for t in reads:
            if t in self.lastw:
                add(self.lastw[t])
        for t in writes:
            if t in self.lastw:
                add(self.lastw[t])
            for ev in self.readers.get(t, ()):
                add(ev)
        out = []
        for k, v in deps.items():
            if self.seen[eng].get(k, 0) < v:
                self.seen[eng][k] = v
                out.append((k, v))
        return out

    def _record(self, ev, reads, writes):
        for t in reads:
            self.readers.setdefault(t, []).append(ev)
        for t in writes:
            self.lastw[t] = ev
            self.readers[t] = []

    def op(self, eng, fn, reads=(), writes=(), track=True):
        assert track or eng == 'pe'
        waits = self._deps(eng, reads, writes)
        ev = (eng, self.cnt[eng] + 1)
        if track:
            self.cnt[eng] += 1
        self.ops[eng].append((waits, fn, eng if track else None, 1))
        self._record(ev, reads, writes)

    def raw(self, eng, fn):
        self.ops[eng].append(([], fn, None, 0))

    def custom(self, q, fn, reads=(), writes=(), key=None):
        assert key not in self.dsem
        self.dsem[key] = self.es.enter_context(self.nc.semaphore("c_" + str(key)))
        waits = self._deps(q, reads, writes)
        self.dcnt[key] = 1
        self.ops[q].append((waits, fn, key, None))
        self._record((key, 1), reads, writes)

    def barrier(self):
        for e in ENGS:
            waits = []
            for f in ENGS:
                if f != e and self.cnt[f] > self.seen[e].get(f, 0):
                    waits.append((f, self.cnt[f]))
                    self.seen[e][f] = self.cnt[f]
            for k, v in self.dcnt.items():
                if self.seen[e].get(k, 0) < v:
                    waits.append((k, v))
                    self.seen[e][k] = v
            self.ops[e].append((waits, None, None, 0))
        self.lastw.clear()
        self.readers.clear()

    def dma(self, q, out, in_, reads=(), writes=(), key=None, in_fn=None, **kw):
        if key not in self.dsem:
            self.dsem[key] = self.es.enter_context(self.nc.semaphore("d_" + str(key)))
            self.dcnt[key] = 0
        waits = self._deps(q, reads, writes)
        self.dcnt[key] += 16
        ev = (key, self.dcnt[key])
        if in_fn is None:
            self.ops[q].append((waits, (lambda e: e.dma_start(out=out, in_=in_, **kw)), key, 16))
        else:
            self.ops[q].append((waits, (lambda e: e.dma_start(out=out, in_=in_fn(), **kw)), key, 16))
        self._record(ev, reads, writes)

    def wait_all(self, eng, keys):
        waits = [(k, self.dcnt[k]) for k in keys]
        self.ops[eng].append((waits, None, None, 0))

    def emit(self):
        nc = self.nc
        self.wait_all('sp', list(self.dcnt.keys()))
        with nc.Block() as block:
            def run(name):
                def body(e):
                    for waits, fn, inc, amt in self.ops[name]:
                        for k, v in waits:
                            e.wait_ge(self._handle(k), v)
                        if fn is None:
                            continue
                        ins = fn(e)
                        if inc is not None:
                            if amt is None:
                                ins.then_inc(self._handle(inc))
                            else:
                                ins.then_inc(self._handle(inc), amt)
                return body
            block.tensor(run('pe'))
            block.scalar(run('act'))
            block.vector(run('dve'))
            block.gpsimd(run('pool'))
            block.sync(run('sp'))
        self.ops = {e: [] for e in ENGS}


F32 = mybir.dt.float32
BF16 = mybir.dt.bfloat16
ALU = mybir.AluOpType
AF = mybir.ActivationFunctionType

D = 1024
DFF = 4096
NT = 256
NBLK = 16
TOK = NT * NBLK
HALO = 16
EPS = 1e-6
BIG = 30000.0


class Ctx:
    def __init__(self, nc, es):
        self.nc = nc
        self.es = es
        self.S = Sched(nc, es)
        self.psn = 0
        self.banks = []
        self.uid = 0
        self.prefix = ""
        self.ges = es
        self.castn = 0

    def sb(self, name, shape, dt):
        return self.es.enter_context(self.nc.sbuf_tensor("sb_" + self.prefix + name, shape, dt))

    def init_psum(self, pools=None):
        if not self.banks:
            for i in range(8):
                self.banks.append(self.ges.enter_context(self.nc.psum_tensor(f"psb{i}", [128, 512], F32)))
        self.pools = pools or {'main': list(range(8))}
        self.pcnt = {k: 0 for k in self.pools}

    def ps(self, n=512, pool='main'):
        ids = self.pools[pool]
        i = ids[self.pcnt[pool] % len(ids)]
        self.pcnt[pool] += 1
        return self.banks[i][:, 0:n], f"ps{i}"

    def tok(self, base):
        self.uid += 1
        return f"{base}#{self.uid}"


class Fz:
    def __init__(self, nc, C, T):
        self.nc, self.C, self.T = nc, C, T


def _begin(fz, es, pools=None, prefix=""):
    if fz is None:
        nc = bass.Bass("TRN2", target_bir_lowering=False)
        C = Ctx(nc, es)

        def decl(name, shape, dtype, kind=None):
            if kind is None:
                return nc.dram_tensor(name, shape, dtype).ap()
            return nc.dram_tensor(name, shape, dtype, kind=kind).ap()
    else:
        nc, C = fz.nc, fz.C
        C.es = es
        C.prefix = prefix

        def decl(name, shape, dtype, kind=None):
            return fz.T.get(prefix + name)
    C.init_psum(pools)
    return nc, C, decl


def _end(fz, C):
    if fz is not None:
        C.S.barrier()
    C.S.emit()


def emit_consts(C):
    S = C.S
    C.ones_bf = C.sb("ones_bf", [128, 128], BF16)
    S.op('pool', lambda e: e.memset(C.ones_bf[:], 1.0), writes=['ones_bf'])


def emit_mod(C, cT_d, wmod_d, bmod_d, groups, out_tile, stage):
    S = C.S
    cact = C.sb(C.tok("cact"), [128, 8], F32)
    bm = C.sb(C.tok("bm"), [128, 48], F32)
    ctok = C.tok("cact")
    btok = C.tok("bm")
    S.dma('sp', cact[:], cT_d[:, :], writes=[ctok], key=ctok)
    S.dma('sp', bm[:], bmod_d[:, :], writes=[btok], key=btok)
    S.op('act', lambda e: e.activation(out=cact[:], in_=cact[:], func=AF.Silu), reads=[ctok], writes=[ctok])
    pi = 0
    for gi, g in enumerate(groups):
        ps, pst = C.ps(512)
        for dc in range(8):
            st, sttoks = stage[pi % 2]
            pi += 1
            S.dma('sp' if pi % 2 else 'act', st[:], wmod_d[dc * 128:(dc + 1) * 128, g * 1024:(g + 1) * 1024], writes=sttoks, key=sttoks[0] + '_m')
            for mch in range(8):
                S.op('pe', (lambda e, ps=ps, st=st, dc=dc, mch=mch: e.matmul(
                    ps[:, mch:mch + 1], lhsT=st[:, mch * 128:(mch + 1) * 128], rhs=cact[:, dc:dc + 1],
                    start=(dc == 0 and mch == 0), stop=(dc == 7 and mch == 7))),
                    reads=sttoks + [ctok], writes=[pst], track=(mch == 7))
        S.op('dve', (lambda e, ps=ps, gi=gi, g=g: e.tensor_tensor(
            out=out_tile[:, gi, :], in0=ps[:, 0:8], in1=bm[:, g * 8:(g + 1) * 8], op=ALU.add)),
            reads=[pst, btok], writes=['mod'])


def emit_cast_weight(C, dst, dsttok, src_views, stage, engines=('dve', 'act', 'dve', 'pool')):
    S = C.S
    quarters = []
    for st, toks in stage:
        quarters.append((st[:, 0:512], toks[0]))
        quarters.append((st[:, 512:1024], toks[1]))
    pieces = []
    for (src, dview, kind) in src_views:
        if kind == '2d':
            for hh in range(2):
                pieces.append((src[:, hh * 512:(hh + 1) * 512], dview[:, hh * 512:(hh + 1) * 512], None))
        else:
            pieces.append((src, dview, kind))
    for (src, dview, kind) in pieces:
        n = C.castn
        C.castn += 1
        qv, qtok = quarters[n % 4]
        sv = qv if kind is None else kind(qv)
        S.dma('sp' if n % 2 == 0 else 'act', sv, src, writes=[qtok], key=qtok)
        eng = engines[n % len(engines)]
        if eng == 'act':
            S.op('act', (lambda e, dview=dview, sv=sv: e.activation(out=dview, in_=sv, func=AF.Copy)), reads=[qtok], writes=[dsttok])
        else:
            S.op(eng, (lambda e, dview=dview, sv=sv: e.tensor_copy(out=dview, in_=sv)), reads=[qtok], writes=[dsttok])


def load_ffn_weights(C, wup_d, wdn_d, stage):
    wup = C.sb("wup_bf", [128, 8, DFF], BF16)
    wdn = C.sb("wdn_bf", [128, 32, D], BF16)
    views = []
    for dc in range(8):
        for q in range(4):
            src = wup_d[dc * 128:(dc + 1) * 128, q * 1024:(q + 1) * 1024]
            views.append((src, wup[:, dc, q * 1024:(q + 1) * 1024], '2d'))
    emit_cast_weight(C, wup, "wup_bf", views, stage)
    views = []
    for fc in range(32):
        src = wdn_d[fc * 128:(fc + 1) * 128, :]
        views.append((src, wdn[:, fc, :], '2d'))
    emit_cast_weight(C, wdn, "wdn_bf", views, stage)
    return wup, wdn


def emit_norm(C, xt, xtok, ncols, A, B, col, out_fn, bufs, rtag):
    for _ in emit_norm_gen(C, xt, xtok, ncols, A, B, col, out_fn, bufs, rtag):
        pass


def emit_norm_gen(C, xt, xtok, ncols, A, B, col, out_fn, bufs, rtag):
    S = C.S
    xsq, rstd = bufs['xsq'], bufs['rstd']
    sq = rstd
    for c in range(8):
        S.op('act', (lambda e, c=c: e.activation(out=xsq[:, c, 0:ncols], in_=xt[:, c, 0:ncols], func=AF.Square)),
             reads=[xtok], writes=['xsq'])
    yield
    ps, pst = C.ps(512)
    for c in range(8):
        S.op('pe', (lambda e, c=c: e.matmul(ps[:, 0:ncols], lhsT=C.ones_bf[:], rhs=xsq[:, c, 0:ncols],
                                            start=(c == 0), stop=(c == 7))),
             reads=['xsq', 'ones_bf'], writes=[pst], track=(c == 7))
    S.op('act', lambda e: e.activation(out=sq[:, 0:ncols], in_=ps[:, 0:ncols], func=AF.Sqrt, bias=EPS, scale=1.0 / D),
         reads=[pst], writes=['rstd'])
    S.op('dve', lambda e: e.reciprocal(out=rstd[:, 0:ncols], in_=sq[:, 0:ncols]), reads=['rstd'], writes=['rstd'])
    for c in range(8):
        tmp, ttok = bufs['tmp'][c % 2]
        S.op('dve', (lambda e, c=c, tmp=tmp: e.tensor_tensor(out=tmp[:, 0:ncols], in0=xt[:, c, 0:ncols],
                                                               in1=rstd[:, 0:ncols], op=ALU.mult)),
             reads=[xtok, 'rstd'], writes=[ttok])
        o, otok = out_fn(c)
        S.op('act', (lambda e, c=c, tmp=tmp, o=o: e.activation(out=o, in_=tmp[:, 0:ncols], func=AF.Identity,
                                                                bias=B[:, col[1] + c:col[1] + c + 1],
                                                                scale=A[:, col[0] + c:col[0] + c + 1])),
             reads=[ttok, 'modc'], writes=[otok])


def ffn_up_gen(C, wup, h2, a2, rl):
    S = C.S
    for f2 in range(16):
        ps, pst = C.ps(512)
        for k in range(2):
            fc = 2 * f2 + k
            for dc in range(8):
                S.op('pe', (lambda e, ps=ps, dc=dc, fc=fc, k=k: e.matmul(
                    ps[:, k * NT:(k + 1) * NT], lhsT=wup[:, dc, fc * 128:(fc + 1) * 128],
                    rhs=h2[:, dc, :], start=(dc == 0), stop=(dc == 7))),
                    reads=['h2', 'wup_bf'], writes=[pst], track=(dc == 7 and k == 1))
        r, rtok = rl[f2 % 2]
        S.op('act', (lambda e, ps=ps, r=r: e.activation(out=r, in_=ps, func=AF.Relu)), reads=[pst], writes=[rtok])
        S.op('dve', (lambda e, r=r, f2=f2: e.tensor_tensor(
            out=a2[:, 2 * f2:2 * f2 + 2, :], in0=r.rearrange("p (a b) -> p a b", a=2),
            in1=r.rearrange("p (a b) -> p a b", a=2), op=ALU.mult)),
            reads=[rtok], writes=[f'a2_{f2}'])
        yield


def ffn_down_gen(C, xt, xtok, xoff, wdn, a2, G2, g2col):
    S = C.S
    for d2 in range(4):
        ps, pst = C.ps(512)
        for k in range(2):
            dc = 2 * d2 + k
            for fc in range(32):
                S.op('pe', (lambda e, ps=ps, dc=dc, fc=fc, k=k: e.matmul(
                    ps[:, k * NT:(k + 1) * NT], lhsT=wdn[:, fc, dc * 128:(dc + 1) * 128],
                    rhs=a2[:, fc, :], start=(fc == 0), stop=(fc == 31))),
                    reads=[f'a2_{fc // 2}', 'wdn_bf'], writes=[pst], track=(fc == 31 and k == 1))
            yield
        for k in range(2):
            dc = 2 * d2 + k
            S.op('dve', (lambda e, ps=ps, dc=dc, k=k: e.scalar_tensor_tensor(
                out=xt[:, dc, xoff:xoff + NT], in0=ps[:, k * NT:(k + 1) * NT], scalar=G2[:, g2col + dc:g2col + dc + 1],
                in1=xt[:, dc, xoff:xoff + NT], op0=ALU.mult, op1=ALU.add)),
                reads=[pst, xtok, 'modc'], writes=[xtok])
        yield


def interleave(main, side, every=1):
    n = 0
    for _ in main:
        n += 1
        if side is not None and n % every == 0:
            if next(side, 'done') == 'done':
                side = None
    if side is not None:
        for _ in side:
            pass


def build_p1(ntiles=NBLK, stop=99, fz=None):
    with ExitStack() as es:
        nc, C, decl = _begin(fz, es, None, "p1_")
        xTh = decl("xTh", [D, NBLK, HALO + NT], F32, "ExternalInput")
        cT = decl("cT", [128, 8], F32, "ExternalInput")
        wmod = decl("wmod", [D, 6 * D], F32, "ExternalInput")
        bmod = decl("bmod", [128, 48], F32, "ExternalInput")
        nrm = decl("nrm", [128, 24], F32, "ExternalInput")
        hval = decl("hval", [128, NBLK], F32, "ExternalInput")
        invc = decl("invc", [128, 8, HALO], F32, "ExternalInput")
        wpool = decl("wpool", [4, 256, 256], F32, "ExternalInput")
        wup_d = decl("wup", [D, DFF], F32, "ExternalInput")
        wdn_d = decl("wdn", [DFF, D], F32, "ExternalInput")
        x1T = decl("x1T", [D, TOK], F32, "ExternalOutput")
        W = HALO + NT
        S = C.S
        emit_consts(C)
        stage = [(C.sb("stg0", [128, 1024], F32), ["stg0a", "stg0b"]), (C.sb("stg1", [128, 1024], F32), ["stg1a", "stg1b"])]
        mod = C.sb("mod", [128, 6, 8], F32)
        nrm_t = C.sb("nrm_t", [128, 24], F32)
        hv = C.sb("hv", [128, NBLK], F32)
        ic = C.sb("ic", [128, 8, HALO], F32)
        S.dma('sp', nrm_t[:], nrm[:, :], writes=['nrm_t'], key='ld_nrm')
        S.dma('sp', hv[:], hval[:, :], writes=['hv'], key='ld_hv')
        S.dma('sp', ic[:], invc[:, :, :], writes=['ic'], key='ld_ic')
        emit_mod(C, cT, wmod, bmod, [0, 1, 2, 3, 4, 5], mod, stage)
        modc = C.sb("modc", [128, 48], F32)
        mt = 'mod'
        S.op('dve', lambda e: e.scalar_tensor_tensor(out=modc[:, 0:8], in0=mod[:, 1, :], scalar=1.0, in1=nrm_t[:, 0:8],
                                                     op0=ALU.add, op1=ALU.mult), reads=[mt, 'nrm_t'], writes=['modc'])
        S.op('dve', lambda e: e.tensor_copy(out=modc[:, 8:16], in_=mod[:, 0, :]), reads=[mt, 'modc'], writes=['modc'])
        S.op('dve', lambda e: e.tensor_tensor(out=modc[:, 16:24], in0=mod[:, 2, :], in1=nrm_t[:, 16:24], op=ALU.mult),
             reads=[mt, 'nrm_t', 'modc'], writes=['modc'])
        S.op('dve', lambda e: e.scalar_tensor_tensor(out=modc[:, 24:32], in0=mod[:, 4, :], scalar=1.0, in1=nrm_t[:, 8:16],
                                                     op0=ALU.add, op1=ALU.mult), reads=[mt, 'nrm_t', 'modc'], writes=['modc'])
        S.op('dve', lambda e: e.tensor_copy(out=modc[:, 32:40], in_=mod[:, 3, :]), reads=[mt, 'modc'], writes=['modc'])
        S.op('dve', lambda e: e.tensor_copy(out=modc[:, 40:48], in_=mod[:, 5, :]), reads=[mt, 'modc'], writes=['modc'])
        if stop == 1:
            S.dma('sp', x1T[0:128, 0:48], modc[:], reads=['modc'], key='st_dbg')
            S.wait_all('sp', ['st_dbg'])
            S.emit()
            return nc
        wp = C.sb("wp_bf", [128, 4, 2, 256], BF16)
        views = []
        for g in range(4):
            src = wpool[g].rearrange("(a p) o -> p a o", p=128)
            views.append((src, wp[:, g, :, :], (lambda qv: qv.rearrange("p (a b) -> p a b", a=2))))
        emit_cast_weight(C, wp, "wp_bf", views, stage)
        wup, wdn = load_ffn_weights(C, wup_d, wdn_d, stage)
        if stop == 2:
            S.dma('sp', x1T[0:128, 0:48], modc[:], reads=['modc', 'wup_bf', 'wdn_bf', 'wp_bf'], key='st_dbg')
            S.wait_all('sp', ['st_dbg'])
            S.emit()
            return nc
        xts = [(C.sb(f"xt{i}", [128, 8, W], F32), f"xt{i}") for i in range(2)]
        hf = C.sb("hf", [128, 8, W], F32)
        SA = C.sb("SA", [128, 2, W], F32)
        SB = C.sb("SB", [128, 2, W], F32)
        h2 = C.sb("h2", [128, 8, NT], BF16)
        a2 = C.sb("a2", [128, 32, NT], BF16)
        bufs = dict(xsq=C.sb("xsq", [128, 8, W], BF16), rstd=C.sb("rstd", [128, W], F32),
                    tmp=[(C.sb("tmpa", [128, W], F32), "tmpa"), (C.sb("tmpb", [128, W], F32), "tmpb")])
        rl = [(C.sb("rla", [128, 2 * NT], F32)[:], "rla"), (C.sb("rlb", [128, 2 * NT], F32)[:], "rlb")]
        fix = C.sb("fix", [128, 2, HALO], F32)
        pl = C.sb("pl", [128, 8, NT], BF16)

        def front_a_gen(t):
            xt, xtok = xts[t % 2]
            src = xTh[:, t, :].rearrange("(a p) w -> p a w", p=128)
            S.dma('sp', xt[:], src, writes=[xtok], key=xtok)
            yield from emit_norm_gen(C, xt, xtok, W, modc, modc, (0, 8), lambda c: (hf[:, c, :], 'hf'), bufs, 'n1')
            yield
            S.op('dve', (lambda e, t=t: e.tensor_scalar(out=hf[:, :, 0:HALO], in0=hf[:, :, 0:HALO], scalar1=hv[:, t:t + 1],
                                                       scalar2=None, op0=ALU.mult)), reads=['hf', 'hv'], writes=['hf'])
            yield
            for g in range(4):
                w = 2 << g
                hg = hf[:, 2 * g:2 * g + 2, :]
                cur, curtok = hg, 'hf'
                sh = 1
                k = 0
                while sh < w:
                    dst, dtok = (SA, 'SA') if k % 2 == 0 else (SB, 'SB')
                    lo = 2 * sh - 1
                    S.op('pool', (lambda e, dst=dst, cur=cur, sh=sh, lo=lo: e.tensor_tensor(
                        out=dst[:, :, lo:W], in0=cur[:, :, lo:W], in1=cur[:, :, lo - sh:W - sh], op=ALU.add)),
                        reads=[curtok], writes=[dtok])
                    cur, curtok = dst, dtok
                    sh *= 2
                    k += 1
                S.op('dve', (lambda e, g=g, cur=cur, w=w, hg=hg: e.scalar_tensor_tensor(
                    out=pl[:, 2 * g:2 * g + 2, :], in0=cur[:, :, HALO:W], scalar=1.0 / w, in1=hg[:, :, HALO:W],
                    op0=ALU.mult, op1=ALU.subtract)), reads=[curtok, 'hf'], writes=['pl'])
                if t == 0:
                    S.op('dve', (lambda e, g=g, cur=cur: e.tensor_tensor(
                        out=fix[:], in0=cur[:, :, HALO:2 * HALO], in1=ic[:, 2 * g:2 * g + 2, :], op=ALU.mult)),
                        reads=[curtok, 'ic'], writes=['fix'])
                    S.op('dve', (lambda e, g=g, hg=hg: e.tensor_tensor(
                        out=pl[:, 2 * g:2 * g + 2, 0:HALO], in0=fix[:], in1=hg[:, :, HALO:2 * HALO], op=ALU.subtract)),
                        reads=['fix', 'hf', 'pl'], writes=['pl'])
                yield
            yield
            for g in range(4):
                for oc in range(2):
                    ps, pst = C.ps(256)
                    for icn in range(2):
                        S.op('pe', (lambda e, ps=ps, g=g, oc=oc, icn=icn: e.matmul(
                            ps, lhsT=wp[:, g, icn, oc * 128:(oc + 1) * 128], rhs=pl[:, 2 * g + icn, :],
                            start=(icn == 0), stop=(icn == 1))), reads=['pl', 'wp_bf'], writes=[pst], track=(icn == 1))
                    dc = 2 * g + oc
                    S.op('dve', (lambda e, ps=ps, dc=dc, xt=xt: e.scalar_tensor_tensor(
                        out=xt[:, dc, HALO:W], in0=ps, scalar=modc[:, 16 + dc:17 + dc], in1=xt[:, dc, HALO:W],
                        op0=ALU.mult, op1=ALU.add)), reads=[pst, xtok, 'modc'], writes=[xtok])
            yield

        def front_b_gen(t):
            xt, xtok = xts[t % 2]
            yield from emit_norm_gen(C, xt[:, :, HALO:W], xtok, NT, modc, modc, (24, 32), lambda c: (h2[:, c, :], 'h2'), bufs, 'n2')

            yield

        def tile_ffn(t):
            xt, xtok = xts[t % 2]
            interleave(ffn_up_gen(C, wup, h2, a2, rl), front_a_gen(t + 1) if t + 1 < ntiles else None, every=2)
            interleave(ffn_down_gen(C, xt, xtok, HALO, wdn, a2, modc, 40), front_b_gen(t + 1) if t + 1 < ntiles else None)
            dst = x1T[:, t * NT:(t + 1) * NT].rearrange("(a p) n -> p a n", p=128)
            S.dma('sp', dst, xt[:, :, HALO:W], reads=[xtok], key=f'st_out{t % 2}')
        for _ in front_a_gen(0):
            pass
        for _ in front_b_gen(0):
            pass
        for t in range(ntiles):
            tile_ffn(t)
        S.wait_all('sp', [f'st_out{i}' for i in range(min(2, ntiles))])
        _end(fz, C)
    return nc


def build_p2(ntiles=NBLK, fz=None):
    with ExitStack() as es:
        nc, C, decl = _begin(fz, es, None, "p2_")
        x1T = decl("x1T", [D, TOK], F32, "ExternalInput")
        cT = decl("cT", [128, 8], F32, "ExternalInput")
        wmod = decl("wmod", [D, 6 * D], F32, "ExternalInput")
        bmod = decl("bmod", [128, 48], F32, "ExternalInput")
        nrm = decl("nrm", [128, 8], F32, "ExternalInput")
        wqkv = decl("wqkv", [D, 3 * D], F32, "ExternalInput")
        qT = decl("qT", [D, TOK], BF16, "ExternalOutput")
        kT = decl("kT", [D, TOK], BF16, "ExternalOutput")
        vO = decl("v", [TOK, D], BF16, "ExternalOutput")
        kmO = decl("kmean", [D, NBLK], F32, "ExternalOutput")
        S = C.S
        emit_consts(C)
        stage = [(C.sb("stg0", [128, 1024], F32), ["stg0a", "stg0b"]), (C.sb("stg1", [128, 1024], F32), ["stg1a", "stg1b"])]
        mod = C.sb("mod", [128, 2, 8], F32)
        nrm_t = C.sb("nrm_t", [128, 8], F32)
        S.dma('sp', nrm_t[:], nrm[:, :], writes=['nrm_t'], key='ld_nrm')
        emit_mod(C, cT, wmod, bmod, [0, 1], mod, stage)
        modc = C.sb("modc", [128, 16], F32)
        S.op('dve', lambda e: e.scalar_tensor_tensor(out=modc[:, 0:8], in0=mod[:, 1, :], scalar=1.0, in1=nrm_t[:, 0:8],
                                                     op0=ALU.add, op1=ALU.mult), reads=['mod', 'nrm_t'], writes=['modc'])
        S.op('dve', lambda e: e.tensor_copy(out=modc[:, 8:16], in_=mod[:, 0, :]), reads=['mod', 'modc'], writes=['modc'])
        wq = C.sb("wqkv_bf", [128, 8, 3 * D], BF16)
        views = []
        for dc in range(8):
            for q in range(3):
                src = wqkv[dc * 128:(dc + 1) * 128, q * 1024:(q + 1) * 1024]
                views.append((src, wq[:, dc, q * 1024:(q + 1) * 1024], '2d'))
        emit_cast_weight(C, wq, "wqkv_bf", views, stage)
        xts = [(C.sb(f"xt{i}", [128, 8, NT], F32), f"xt{i}") for i in range(2)]
        hs = [(C.sb(f"h2{i}", [128, 8, NT], BF16), f"h2{i}") for i in range(2)]
        qts = [(C.sb(f"qt{i}", [128, 8, NT], BF16), f"qt{i}") for i in range(2)]
        kts = [(C.sb(f"kt{i}", [128, 8, NT], BF16), f"kt{i}") for i in range(2)]
        vts = [(C.sb(f"vt{i}", [128, 2, D], BF16), f"vt{i}") for i in range(2)]
        km = C.sb("km", [128, 8, NBLK], F32)
        S.op('pool', lambda e: e.memset(km[:], 0.0), writes=['km'])
        bufs = dict(xsq=C.sb("xsq", [128, 8, NT], BF16), rstd=C.sb("rstd", [128, NT], F32),
                    tmp=[(C.sb("tmpa", [128, NT], F32), "tmpa"), (C.sb("tmpb", [128, NT], F32), "tmpb")])
        def norm_gen(t):
            xt, xtok = xts[t % 2]
            src = x1T[:, t * NT:(t + 1) * NT].rearrange("(a p) n -> p a n", p=128)
            S.dma('sp', xt[:], src, writes=[xtok], key=xtok)
            h, htok = hs[t % 2]
            yield from emit_norm_gen(C, xt, xtok, NT, modc, modc, (0, 8), lambda c: (h[:, c, :], htok), bufs, 'n1')
            yield

        def qkv_gen(t):
            h, htok = hs[t % 2]
            qt, qtok = qts[t % 2]
            kt, ktok = kts[t % 2]
            vt, vtok = vts[t % 2]
            for which in range(2):
                for o2 in range(4):
                    ps, pst = C.ps(512)
                    for k in range(2):
                        oc = 2 * o2 + k
                        col = which * D + oc * 128
                        for dc in range(8):
                            S.op('pe', (lambda e, ps=ps, dc=dc, col=col, k=k: e.matmul(
                                ps[:, k * NT:(k + 1) * NT], lhsT=wq[:, dc, col:col + 128], rhs=h[:, dc, :],
                                start=(dc == 0), stop=(dc == 7))),
                                reads=[htok, 'wqkv_bf'], writes=[pst], track=(dc == 7 and k == 1))
                    psv = ps.rearrange("p (a b) -> p a b", a=2)
                    if which == 0:
                        S.op('dve', (lambda e, psv=psv, o2=o2, qt=qt: e.tensor_scalar(
                            out=qt[:, 2 * o2:2 * o2 + 2, :], in0=psv, scalar1=0.125, scalar2=None, op0=ALU.mult)),
                            reads=[pst], writes=[qtok])
                    else:
                        for k in range(2):
                            oc = 2 * o2 + k
                            S.op('act', (lambda e, ps=ps, oc=oc, kt=kt, t=t, k=k: e.activation(
                                out=kt[:, oc, :], in_=ps[:, k * NT:(k + 1) * NT], func=AF.Copy, accum_out=km[:, oc, t:t + 1])),
                                reads=[pst, 'km'], writes=[ktok, 'km'])
                    yield
            for ts_ in range(2):
                for fh in range(2):
                    ps, pst = C.ps(512)
                    for dc in range(8):
                        S.op('pe', (lambda e, ps=ps, dc=dc, ts_=ts_, fh=fh: e.matmul(
                            ps, lhsT=h[:, dc, ts_ * 128:(ts_ + 1) * 128],
                            rhs=wq[:, dc, 2 * D + fh * 512:2 * D + (fh + 1) * 512],
                            start=(dc == 0), stop=(dc == 7))),
                            reads=[htok, 'wqkv_bf'], writes=[pst], track=(dc == 7))
                    S.op('dve', (lambda e, ps=ps, ts_=ts_, fh=fh, vt=vt: e.tensor_copy(
                        out=vt[:, ts_, fh * 512:(fh + 1) * 512], in_=ps)), reads=[pst], writes=[vtok])
                    yield
            S.dma('sp', qT[:, t * NT:(t + 1) * NT].rearrange("(a p) n -> p a n", p=128), qt[:], reads=[qtok], key=f'stq{t % 2}')
            if fz is None:
                S.dma('sp', kT[:, t * NT:(t + 1) * NT].rearrange("(a p) n -> p a n", p=128), kt[:], reads=[ktok], key=f'stk{t % 2}')
                S.dma('sp', vO[t * NT:(t + 1) * NT, :].rearrange("(a p) d -> p a d", p=128), vt[:], reads=[vtok], key=f'stv{t % 2}')
            else:
                for a in range(4):
                    S.dma('sp', fz.T["own_k"][a].ap()[:, t * NT:(t + 1) * NT].rearrange("(c p) n -> p c n", p=128),
                          kt[:, 2 * a:2 * a + 2, :], reads=[ktok], key=f'stk{t % 2}')
                S.dma('sp', fz.T["own_v"][t // 4].ap()[(t % 4) * NT:(t % 4 + 1) * NT, :].rearrange("(a p) d -> p a d", p=128),
                      vt[:], reads=[vtok], key=f'stv{t % 2}')
            yield

        for _ in norm_gen(0):
            pass
        for t in range(ntiles):
            interleave(qkv_gen(t), norm_gen(t + 1) if t + 1 < ntiles else None, every=2)
        S.op('dve', lambda e: e.tensor_scalar(out=km[:], in0=km[:], scalar1=1.0 / NT, scalar2=None, op0=ALU.mult),
             reads=['km'], writes=['km'])
        S.dma('sp', kmO[:, :].rearrange("(a p) n -> p a n", p=128), km[:], reads=['km'], key='stkm')
        nk = min(2, ntiles)
        S.wait_all('sp', [f'stq{i}' for i in range(nk)] + [f'stk{i}' for i in range(nk)] + [f'stv{i}' for i in range(nk)] + ['stkm'])
        _end(fz, C)
    return nc


NSLOT = 32
LF = 1792
LX = 1664


def build_p3(nheads=16, nblk=NBLK, fz=None):
    with ExitStack() as es:
        nc, C, decl = _begin(fz, es, {'S': [0, 1, 2, 3], 'acc': [4, 5], 'g': [6, 7]}, "p3_")
        qT = decl("qT", [D, TOK], BF16, "ExternalInput")
        kTs = decl("kTs", [D, NSLOT * NT], BF16, "ExternalInput")
        vS = decl("vS", [NSLOT * NT, D], BF16, "ExternalInput")
        kmS = decl("kmS", [D, NSLOT], F32, "ExternalInput")
        pm_d = decl("pm", [128, NBLK, NSLOT], F32, "ExternalInput")
        cfar_d = decl("cfar", [128, NBLK, NSLOT], F32, "ExternalInput")
        cown_d = decl("cown", [128, NBLK, NSLOT], F32, "ExternalInput")
        rb31_d = decl("rb31", [128, 16], F32, "ExternalInput")
        rbaug_d = decl("rbaug", [33, 16], F32, "ExternalInput")
        onehot_d = decl("onehot", [33, LF], F32, "ExternalInput")
        ind_d = decl("ind", [32, NSLOT * NT], BF16, "ExternalInput")
        jmat_d = decl("jmat", [128, 256], BF16, "ExternalInput")
        oT = decl("oT", [D, TOK], BF16, "ExternalOutput")
        Fd = nc.dram_tensor("Fd", [16, LF], BF16)
        S = C.S
        dyn = {}
        if False:
            KN = fz.T["KN"].ap()
            VN3t = fz.T["VN3"].ap().transpose([1, 0, 2])
            KMN = fz.T["KMN"].ap()
            hv_d = fz.T["hv"]

            def _setup(e):
                r1 = e.alloc_register("r_blk")
                r2 = e.alloc_register("r_tile")
                e.reg_load(r1, hv_d[0:1, 0:1])
                e.reg_load(r2, hv_d[0:1, 1:2])
                dyn['blk'] = e.snap(r1, min_val=0, max_val=1)
                dyn['tile'] = e.snap(r2, min_val=0, max_val=2)
            S.raw('sp', _setup)
        rbaug = C.sb("rbaug", [33, 16], F32)
        onehot = C.sb("onehot", [33, LF], F32)
        Fsb = C.sb("Fsb", [16, LF], BF16)
        S.dma('sp', rbaug[:], rbaug_d[:, :], writes=['rbaug'], key='ld_rbaug')
        S.dma('sp', onehot[:], onehot_d[:, :], writes=['onehot'], key='ld_onehot')
        for k in range(4):
            ps, pst = C.ps(512, 'g')
            wd = min(512, LF - k * 512)
            S.op('pe', (lambda e, ps=ps, k=k, wd=wd: e.matmul(ps[0:16, 0:wd], lhsT=rbaug[:], rhs=onehot[:, k * 512:k * 512 + wd],
                                                             start=True, stop=True)), reads=['rbaug', 'onehot'], writes=[pst])
            S.op('dve', (lambda e, ps=ps, k=k, wd=wd: e.tensor_copy(out=Fsb[:, k * 512:k * 512 + wd], in_=ps[0:16, 0:wd])),
                 reads=[pst], writes=['Fsb'])
        S.dma('sp', Fd.ap()[:, :], Fsb[:], reads=['Fsb'], writes=['Fd'], key='st_Fd')
        pm = C.sb("pm", [128, NBLK, NSLOT], F32)
        cfar = C.sb("cfar", [128, NBLK, NSLOT], F32)
        cown = C.sb("cown", [128, NBLK, NSLOT], F32)
        rb31 = C.sb("rb31", [128, 16], F32)
        jmat = C.sb("jmat", [128, 256], BF16)
        for nm, t_, d_ in (('pm', pm, pm_d), ('cfar', cfar, cfar_d), ('cown', cown, cown_d)):
            S.dma('sp', t_[:], d_[:, :, :], writes=[nm], key='ld_' + nm)
        S.dma('sp', rb31[:], rb31_d[:, :], writes=['rb31'], key='ld_rb31')
        S.dma('sp', jmat[:], jmat_d[:, :], writes=['jmat'], key='ld_jmat')
        J = jmat[:, 0:128]
        Iden = jmat[:, 128:256]
        kas = [(C.sb(f"ka{i}", [96, NSLOT * NT], BF16), f"ka{i}") for i in range(2)]
        qas = [(C.sb(f"qa{i}", [96, TOK], BF16), f"qa{i}") for i in range(2)]
        vas = [(C.sb(f"va{i}", [128, 2 * NSLOT, 128], BF16), f"va{i}") for i in range(2)]
        Bhs = [(C.sb(f"Bh{i}", [128, LX], BF16), f"Bh{i}") for i in range(2)]
        kmfs = [(C.sb(f"kmf{i}", [64, NSLOT], F32), f"kmf{i}") for i in range(2)]
        kmbs = [(C.sb(f"kmb{i}", [64, NSLOT], BF16), f"kmb{i}") for i in range(2)]
        ohs = [(C.sb(f"oh{i}", [64, TOK], BF16), f"oh{i}") for i in range(2)]
        Chs = [(C.sb(f"Ch{i}", [128, NBLK, NSLOT], F32), f"Ch{i}") for i in range(2)]
        vps = [(C.sb(f"vp{i}", [128, 96], BF16), f"vp{i}") for i in range(2)]
        pTs = [(C.sb(f"pT{i}", [128, 512], BF16), f"pT{i}") for i in range(3)]
        gms = [(C.sb(f"gm{i}", [128, NSLOT], F32), f"gm{i}") for i in range(2)]
        mxs = [(C.sb(f"mx{i}", [128, 8], F32), f"mx{i}") for i in range(2)]
        As = [(C.sb(f"A{i}", [128, NSLOT], F32), f"A{i}") for i in range(2)]
        rds = [(C.sb(f"rd{i}", [128, NT], F32), f"rd{i}") for i in range(2)]
        rd0s = [(C.sb(f"rdz{i}", [64, NT], F32), f"rdz{i}") for i in range(2)]
        for i in range(2):
            S.dma('sp', kas[i][0][64:96, :], ind_d[:, :], writes=[kas[i][1] + 'i'], key=f'ld_ind{i}')
            S.op('pool', (lambda e, i=i: e.memset(vas[i][0][:, :, 64:128], 1.0)), writes=[vas[i][1] + 'o'])
            S.op('pool', (lambda e, i=i: e.memset(vps[i][0][:], 0.0)), writes=[vps[i][1]])
        pTs = pTs + [(C.sb("pT3", [128, 512], BF16), "pT3")]
        state = dict(cnt=0, pcnt=0)

        def bufs_of(h):
            return dict(ka=kas[h % 2], qa=qas[h % 2], va=vas[h % 2], Bh=Bhs[h % 2], kmf=kmfs[h % 2], kmb=kmbs[h % 2],
                        oh=ohs[h % 2], Ch=Chs[h % 2])

        def emit_loads(h):
            B = bufs_of(h)
            ka, katok = B['ka']
            qa, qatok = B['qa']
            va, vatok = B['va']
            Bh, Bhtok = B['Bh']
            kmf, kmftok = B['kmf']
            S.dma('sp', kmf[:], kmS[h * 64:(h + 1) * 64, :], writes=[kmftok], key=kmftok)
            S.dma('sp', qa[0:64, :], qT[h * 64:(h + 1) * 64, :], writes=[qatok], key=qatok)
            S.dma('sp', Bh[:], bass.AP(Fd, h * LF, [[1, 128], [1, LX]]), reads=['Fd'], writes=[Bhtok], key=Bhtok)
            if fz is None:
                S.dma('sp', ka[0:64, :], kTs[h * 64:(h + 1) * 64, :], writes=[katok], key=katok)
                for pc in range(4):
                    src = vS[pc * 2048:(pc + 1) * 2048, h * 64:(h + 1) * 64].rearrange("(a p) d -> p a d", p=128)
                    S.dma('sp', va[:, pc * 16:(pc + 1) * 16, 0:64], src, writes=[vatok], key=vatok)
            else:
                rl = (h % 4) * 64
                gk = fz.T['G_k'][h // 4].ap().rearrange("(e r) (m t) -> r m e t", e=2, t=NT)[rl:rl + 64]
                S.dma('sp', ka[0:64, :].rearrange("p (m e t) -> p m e t", m=16, e=2), gk, writes=[katok], key=katok)
                for pc in range(4):
                    for e_ in range(2):
                        for m_ in range(4):
                            k0 = pc * 16 + m_ * 4 + 2 * e_
                            r0 = e_ * 1024 + m_ * NT
                            src = fz.T['G_v'][pc].ap()[r0:r0 + NT, h * 64:(h + 1) * 64].rearrange("(u p) d -> p u d", p=128)
                            S.dma('sp', va[:, k0:k0 + 2, 0:64], src, writes=[vatok], key=vatok)

        def gate_gen(h):
            B = bufs_of(h)
            qa, qatok = B['qa']
            kmf, kmftok = B['kmf']
            kmb, kmbtok = B['kmb']
            Ch, Chtok = B['Ch']
            qam = qatok + 'm'
            S.op('dve', (lambda e: e.tensor_copy(out=kmb[:], in_=kmf[:])), reads=[kmftok], writes=[kmbtok])
            S.op('dve', (lambda e: e.scalar_tensor_tensor(out=Ch[:], in0=cfar[:], scalar=rb31[:, h:h + 1], in1=cown[:],
                                                          op0=ALU.mult, op1=ALU.add)),
                 reads=['cfar', 'cown', 'rb31'], writes=[Chtok])
            chunks = [(i, ch) for i in range(nblk) for ch in range(2)]
            pend = {}

            def step_a(c):
                i, ch = chunks[c]
                q0 = i * NT + ch * 128
                cnt = state['cnt']
                gm, gmtok = gms[cnt % 2]
                mx, mxtok = mxs[cnt % 2]
                A, Atok = As[cnt % 2]
                vp, vptok = vps[cnt % 2]
                state['cnt'] += 1
                ps, pst = C.ps(512, 'g')
                S.op('pe', (lambda e, ps=ps, q0=q0: e.matmul(
                    ps[:, 0:NSLOT], lhsT=qa[0:64, q0:q0 + 128], rhs=kmb[:], start=True, stop=True)),
                    reads=[qatok, kmbtok], writes=[pst])
                S.op('dve', (lambda e, ps=ps, gm=gm, i=i: e.tensor_tensor(out=gm[:], in0=ps[:, 0:NSLOT], in1=pm[:, i, :], op=ALU.add)),
                     reads=[pst, 'pm'], writes=[gmtok])
                S.op('dve', (lambda e, gm=gm, mx=mx: e.max(out=mx[:], in_=gm[:])), reads=[gmtok], writes=[mxtok])
                S.op('dve', (lambda e, gm=gm, mx=mx, A=A: e.tensor_scalar(out=A[:], in0=gm[:], scalar1=mx[:, 2:3], scalar2=-1.0,
                                                                            op0=ALU.is_ge, op1=ALU.add)),
                     reads=[gmtok, mxtok], writes=[Atok])
                S.op('dve', (lambda e, A=A, vp=vp, i=i: e.scalar_tensor_tensor(
                    out=vp[:, 64:96], in0=A[:], scalar=BIG, in1=Ch[:, i, :], op0=ALU.mult, op1=ALU.add)),
                    reads=[Atok, Chtok], writes=[vptok])
                pend[c] = (vp, vptok, q0)

            def step_b(c):
                vp, vptok, q0 = pend.pop(c)
                ps2, pst2 = C.ps(512, 'g')
                S.op('pe', (lambda e, ps2=ps2, vp=vp: e.matmul(ps2[0:96, 0:128], lhsT=vp[:], rhs=Iden, start=True, stop=True)),
                     reads=[vptok, 'jmat'], writes=[pst2])
                S.op('dve', (lambda e, ps2=ps2, q0=q0: e.tensor_copy(out=qa[64:96, q0:q0 + 128], in_=ps2[64:96, 0:128])),
                     reads=[pst2], writes=[qam])

            step_a(0)
            yield
            for c in range(1, len(chunks)):
                step_a(c)
                yield
                step_b(c - 1)
                yield
            step_b(len(chunks) - 1)
            yield

        def attn_gen(h):
            B = bufs_of(h)
            ka, katok = B['ka']
            qa, qatok = B['qa']
            va, vatok = B['va']
            Bh, Bhtok = B['Bh']
            oh, ohtok = B['oh']
            qam = qatok + 'm'
            items = [(i, s) for i in range(nblk) for s in range(2 * i + 2)]
            N = len(items)
            LAG = 3
            info = {}
            blk = {}

            def emit_qk(n):
                i, s = items[n]
                own = 2 * i + 1
                near = (own - s) <= 5
                sps, spst = C.ps(512, 'S')
                for k2 in range(2):
                    kt = 2 * s + k2
                    S.op('pe', (lambda e, sps=sps, kt=kt, k2=k2, i=i, near=near: e.matmul(
                        sps[:, k2 * NT:(k2 + 1) * NT], lhsT=ka[0:96, kt * 128:(kt + 1) * 128], rhs=qa[0:96, i * NT:(i + 1) * NT],
                        start=True, stop=(not near))),
                        reads=[katok, katok + 'i', qatok, qam], writes=[spst], track=(not near and k2 == 1))
                    if near:
                        x0 = (own - s) * NT - k2 * 128 + 128
                        S.op('pe', (lambda e, sps=sps, x0=x0, k2=k2: e.matmul(
                            sps[:, k2 * NT:(k2 + 1) * NT], lhsT=J, rhs=Bh[:, x0:x0 + NT], start=False, stop=True)),
                            reads=[Bhtok, 'jmat'], writes=[spst], track=(k2 == 1))
                pT, pTtok = pTs[state['pcnt'] % 4]
                state['pcnt'] += 1
                S.op('act', (lambda e, sps=sps, pT=pT: e.activation(out=pT[:], in_=sps, func=AF.Exp)), reads=[spst], writes=[pTtok])
                info[n] = (pT, pTtok)

            def emit_pv(n):
                i, s = items[n]
                own = 2 * i + 1
                pT, pTtok = info.pop(n)
                if s == 0:
                    blk[i] = C.ps(512, 'acc')
                acc, acctok = blk[i]
                nmm = 2 * (own + 1)
                for k2 in range(2):
                    kt = 2 * s + k2
                    mi = 2 * s + k2
                    S.op('pe', (lambda e, acc=acc, pT=pT, kt=kt, k2=k2, mi=mi, nmm=nmm: e.matmul(
                        acc[:, 0:NT], lhsT=va[:, kt, :], rhs=pT[:, k2 * NT:(k2 + 1) * NT], start=(mi == 0), stop=(mi == nmm - 1))),
                        reads=[vatok, vatok + 'o', pTtok], writes=[acctok], track=(k2 == 1))
                if s == own:
                    rd, rdtok = rds[i % 2]
                    rd0, rd0tok = rd0s[i % 2]
                    S.op('dve', (lambda e, acc=acc, rd=rd: e.reciprocal(out=rd[64:128, :], in_=acc[64:128, 0:NT])), reads=[acctok], writes=[rdtok])
                    S.op('dve', (lambda e, rd=rd, rd0=rd0: e.tensor_copy(out=rd0[0:64, :], in_=rd[64:128, :])), reads=[rdtok], writes=[rd0tok])
                    S.op('dve', (lambda e, acc=acc, rd0=rd0, i=i: e.tensor_tensor(out=oh[0:64, i * NT:(i + 1) * NT], in0=acc[0:64, 0:NT],
                                                                               in1=rd0[0:64, :], op=ALU.mult)),
                         reads=[acctok, rd0tok], writes=[ohtok])
                    del blk[i]

            for n in range(N + LAG):
                if n < N:
                    emit_qk(n)
                if n >= LAG:
                    emit_pv(n - LAG)
                yield
            S.dma('sp', oT[h * 64:(h + 1) * 64, 0:nblk * NT], oh[0:64, 0:nblk * NT], reads=[ohtok], key=f'st_o{h % 2}')

        GATE_START = 64 if nblk == NBLK else 2
        emit_loads(0)
        for _ in gate_gen(0):
            pass
        for h in range(nheads):
            gg = None
            if h + 1 < nheads:
                emit_loads(h + 1)
                gg = gate_gen(h + 1)
            step = 0
            for _ in attn_gen(h):
                step += 1
                if gg is not None and step >= GATE_START and step % 3 == 0:
                    if next(gg, 'done') == 'done':
                        gg = None
            if gg is not None:
                for _ in gg:
                    pass
        _end(fz, C)
    return nc


def build_p4(ntiles=NBLK, fz=None):
    with ExitStack() as es:
        nc, C, decl = _begin(fz, es, None, "p4_")
        x1T = decl("x1T", [D, TOK], F32, "ExternalInput")
        oT = decl("oT", [D, TOK], BF16, "ExternalInput")
        cT = decl("cT", [128, 8], F32, "ExternalInput")
        wmod = decl("wmod", [D, 6 * D], F32, "ExternalInput")
        bmod = decl("bmod", [128, 48], F32, "ExternalInput")
        nrm = decl("nrm", [128, 16], F32, "ExternalInput")
        wo_d = decl("wo", [D, D], F32, "ExternalInput")
        wup_d = decl("wup", [D, DFF], F32, "ExternalInput")
        wdn_d = decl("wdn", [DFF, D], F32, "ExternalInput")
        outT = decl("outT", [D, TOK], F32, "ExternalOutput")
        S = C.S
        emit_consts(C)
        stage = [(C.sb("stg0", [128, 1024], F32), ["stg0a", "stg0b"]), (C.sb("stg1", [128, 1024], F32), ["stg1a", "stg1b"])]
        mod = C.sb("mod", [128, 4, 8], F32)
        nrm_t = C.sb("nrm_t", [128, 16], F32)
        S.dma('sp', nrm_t[:], nrm[:, :], writes=['nrm_t'], key='ld_nrm')
        emit_mod(C, cT, wmod, bmod, [2, 3, 4, 5], mod, stage)
        modc = C.sb("modc", [128, 48], F32)
        S.op('pool', lambda e: e.memset(modc[:], 0.0), writes=['modc'])
        S.op('dve', lambda e: e.tensor_copy(out=modc[:, 0:8], in_=mod[:, 0, :]), reads=['mod', 'modc'], writes=['modc'])
        S.op('dve', lambda e: e.scalar_tensor_tensor(out=modc[:, 8:16], in0=mod[:, 2, :], scalar=1.0, in1=nrm_t[:, 0:8],
                                                     op0=ALU.add, op1=ALU.mult), reads=['mod', 'nrm_t', 'modc'], writes=['modc'])
        S.op('dve', lambda e: e.tensor_copy(out=modc[:, 16:24], in_=mod[:, 1, :]), reads=['mod', 'modc'], writes=['modc'])
        S.op('dve', lambda e: e.tensor_copy(out=modc[:, 24:32], in_=mod[:, 3, :]), reads=['mod', 'modc'], writes=['modc'])
        S.op('dve', lambda e: e.tensor_copy(out=modc[:, 32:40], in_=nrm_t[:, 8:16]), reads=['nrm_t', 'modc'], writes=['modc'])
        wo = C.sb("wo_bf", [128, 8, D], BF16)
        views = []
        for fc in range(8):
            src = wo_d[fc * 128:(fc + 1) * 128, :]
            views.append((src, wo[:, fc, :], '2d'))
        emit_cast_weight(C, wo, "wo_bf", views, stage)
        wup, wdn = load_ffn_weights(C, wup_d, wdn_d, stage)
        xts = [(C.sb(f"xt{i}", [128, 8, NT], F32), f"xt{i}") for i in range(2)]
        ots = [(C.sb(f"ot{i}", [128, 8, NT], BF16), f"ot{i}") for i in range(2)]
        h2 = C.sb("h2", [128, 8, NT], BF16)
        a2 = C.sb("a2", [128, 32, NT], BF16)
        bufs = dict(xsq=C.sb("xsq", [128, 8, NT], BF16), rstd=C.sb("rstd", [128, NT], F32),
                    tmp=[(C.sb("tmpa", [128, NT], F32), "tmpa"), (C.sb("tmpb", [128, NT], F32), "tmpb")])
        rl = [(C.sb("rla", [128, 2 * NT], F32)[:], "rla"), (C.sb("rlb", [128, 2 * NT], F32)[:], "rlb")]

        def front_gen(t):
            xt, xtok = xts[t % 2]
            ot, ottok = ots[t % 2]
            S.dma('sp', xt[:], x1T[:, t * NT:(t + 1) * NT].rearrange("(a p) n -> p a n", p=128), writes=[xtok], key=xtok)
            S.dma('act', ot[:], oT[:, t * NT:(t + 1) * NT].rearrange("(a p) n -> p a n", p=128), writes=[ottok], key=ottok)
            for d2 in range(4):
                ps, pst = C.ps(512)
                for k in range(2):
                    dc = 2 * d2 + k
                    for fc in range(8):
                        S.op('pe', (lambda e, ps=ps, dc=dc, fc=fc, k=k, ot=ot: e.matmul(
                            ps[:, k * NT:(k + 1) * NT], lhsT=wo[:, fc, dc * 128:(dc + 1) * 128], rhs=ot[:, fc, :],
                            start=(fc == 0), stop=(fc == 7))), reads=[ottok, 'wo_bf'], writes=[pst], track=(fc == 7 and k == 1))
                for k in range(2):
                    dc = 2 * d2 + k
                    S.op('dve', (lambda e, ps=ps, dc=dc, k=k, xt=xt: e.scalar_tensor_tensor(
                        out=xt[:, dc, :], in0=ps[:, k * NT:(k + 1) * NT], scalar=modc[:, dc:dc + 1], in1=xt[:, dc, :],
                        op0=ALU.mult, op1=ALU.add)), reads=[pst, xtok, 'modc'], writes=[xtok])
                yield
            emit_norm(C, xt, xtok, NT, modc, modc, (8, 16), lambda c: (h2[:, c, :], 'h2'), bufs, 'n2')
            yield

        def back_gen(t):
            xt, xtok = xts[t % 2]
            emit_norm(C, xt, xtok, NT, modc, modc, (32, 40), lambda c: (xt[:, c, :], xtok), bufs, 'nf')
            S.dma('sp', outT[:, t * NT:(t + 1) * NT].rearrange("(a p) n -> p a n", p=128), xt[:], reads=[xtok], key=f'st_out{t % 2}')
            yield

        for _ in front_gen(0):
            pass
        for t in range(ntiles):
            xt, xtok = xts[t % 2]
            interleave(ffn_up_gen(C, wup, h2, a2, rl), back_gen(t - 1) if t > 0 else None, every=3)
            interleave(ffn_down_gen(C, xt, xtok, 0, wdn, a2, modc, 24), front_gen(t + 1) if t + 1 < ntiles else None)
        for _ in back_gen(ntiles - 1):
            pass
        _end(fz, C)
    return nc


def build_px(fz):
    with ExitStack() as es:
        nc, C, decl = _begin(fz, es, None, "px_")
        S = C.S
        T = fz.T
        groups = [[0, 1], [2, 3], [4, 5], [6, 7]]
        pairs = [(T['own_k'][a], T['G_k'][a], f'k{a}') for a in range(4)] + [(T['own_v'][a], T['G_v'][a], f'v{a}') for a in range(4)]
        pairs.append((T['own_km'], T['G_km'], 'km'))
        for own, G, nm in pairs:
            S.custom('pool', (lambda e, own=own, G=G: e.collective_compute(
                "AllGather", ALU.bypass, replica_groups=groups, ins=[own.ap().opt()], outs=[G.ap().opt()])),
                writes=['G_' + nm], key='cc_' + nm)
        if os.environ.get('PX_STOP') == '1':
            _end(fz, C)
            return
        KMN = T['KMN'].ap()
        Gkm = T['G_km'].ap()
        gk = C.sb("gkm", [128, 2, 8, 16], F32)
        kmn = C.sb("kmn", [128, 8, NSLOT], F32)
        S.dma('sp', gk[:], Gkm.rearrange("(e a p) m -> p e a m", e=2, a=8), reads=['G_km'], writes=['gkm'], key='ld_gkm')
        S.op('pool', lambda e: e.memset(kmn[:], 0.0), writes=['kmn'])
        for e_ in range(2):
            S.op('dve', (lambda e, e_=e_: e.tensor_copy(
                out=kmn[:].rearrange("p a (m e) -> p a m e", e=2)[:, :, :, e_], in_=gk[:, e_, :, :])),
                reads=['gkm', 'kmn'], writes=['kmn'])
        S.dma('sp', KMN.rearrange("(a p) s -> p a s", p=128), kmn[:], reads=['kmn'], writes=['KMN'], key='st_kmn')
        _end(fz, C)


def build_fused():
    nc = bass.Bass("TRN2", target_bir_lowering=False)
    I32 = mybir.dt.int32
    with ExitStack() as ges:
        C = Ctx(nc, ges)
        T = {}

        def ext(name, shape, dt, kind="ExternalInput"):
            return nc.dram_tensor(name, shape, dt, kind=kind).ap()
        T["p1_xTh"] = ext("xTh", [D, NBLK, HALO + NT], F32)
        cT = ext("cT", [128, 8], F32)
        wmod0 = ext("wmod0", [D, 6 * D], F32)
        wmod1 = ext("wmod1", [D, 6 * D], F32)
        bmod0 = ext("bmod0", [128, 48], F32)
        bmod1 = ext("bmod1", [128, 48], F32)
        T["p1_nrm"] = ext("nrm1", [128, 24], F32)
        T["p2_nrm"] = ext("nrm2", [128, 8], F32)
        T["p4_nrm"] = ext("nrm4", [128, 16], F32)
        T["p1_hval"] = ext("hval", [128, NBLK], F32)
        T["p1_invc"] = ext("invc", [128, 8, HALO], F32)
        T["p1_wpool"] = ext("wpool", [4, 256, 256], F32)
        T["p1_wup"] = ext("wup0", [D, DFF], F32)
        T["p1_wdn"] = ext("wdn0", [DFF, D], F32)
        T["p2_wqkv"] = ext("wqkv", [D, 3 * D], F32)
        T["p4_wo"] = ext("wo", [D, D], F32)
        T["p4_wup"] = ext("wup1", [D, DFF], F32)
        T["p4_wdn"] = ext("wdn1", [DFF, D], F32)
        T["p3_pm"] = ext("pm", [128, NBLK, NSLOT], F32)
        T["p3_cfar"] = ext("cfar", [128, NBLK, NSLOT], F32)
        T["p3_cown"] = ext("cown", [128, NBLK, NSLOT], F32)
        T["p3_rb31"] = ext("rb31", [128, 16], F32)
        T["p3_rbaug"] = ext("rbaug", [33, 16], F32)
        T["p3_onehot"] = ext("onehot", [33, LF], F32)
        T["p3_ind"] = ext("ind", [32, NSLOT * NT], BF16)
        T["p3_jmat"] = ext("jmat", [128, 256], BF16)
        for p in ("p1_", "p2_", "p4_"):
            T[p + "cT"] = cT
        T["p1_wmod"], T["p1_bmod"] = wmod0, bmod0
        T["p2_wmod"], T["p2_bmod"] = wmod1, bmod1
        T["p4_wmod"], T["p4_bmod"] = wmod1, bmod1
        x1T = nc.dram_tensor("x1T", [D, TOK], F32)
        qT = nc.dram_tensor("qT", [D, TOK], BF16)
        T["own_k"] = [nc.dram_tensor(f"own_k{a}", [256, TOK], BF16) for a in range(4)]
        T["own_v"] = [nc.dram_tensor(f"own_v{a}", [1024, D], BF16) for a in range(4)]
        T["own_km"] = nc.dram_tensor("own_km", [D, NBLK], F32)
        T["G_k"] = [nc.dram_tensor(f"G_k{a}", [512, TOK], BF16) for a in range(4)]
        T["G_v"] = [nc.dram_tensor(f"G_v{a}", [2048, D], BF16) for a in range(4)]
        T["G_km"] = nc.dram_tensor("G_km", [2 * D, NBLK], F32)
        T["KN"] = nc.dram_tensor("KN", [D, NSLOT, NT], BF16)
        T["VN3"] = nc.dram_tensor("VN3", [2 * NSLOT, 128, D], BF16)
        T["KMN"] = nc.dram_tensor("KMN", [D, NSLOT], F32)
        T["p3_kTs"] = T["KN"].ap().rearrange("f s t -> f (s t)")
        T["p3_vS"] = T["VN3"].ap().rearrange("a p d -> (a p) d")
        T["p3_kmS"] = T["KMN"].ap()
        oT = nc.dram_tensor("oT", [D, TOK], BF16)
        T["p1_x1T"] = T["p2_x1T"] = T["p4_x1T"] = x1T.ap()
        T["p2_qT"] = T["p3_qT"] = qT.ap()
        T["p2_kmean"] = T["own_km"].ap()
        T["p3_oT"] = T["p4_oT"] = oT.ap()
        T["p4_outT"] = ext("outT", [D, TOK], F32, kind="ExternalOutput")
        fz = Fz(nc, C, T)
        build_p1(fz=fz)
        build_p2(fz=fz)
        build_px(fz)
        build_p3(fz=fz)
        build_p4(fz=fz)
    return nc


def core_blocks(c):
    b, half = c // 2, c % 2
    return b, half, [2 * i + half for i in range(NBLK)]

def fm(v):
    v = np.asarray(v, np.float32)
    return np.ascontiguousarray(v.reshape(-1, 128).T)

def p1_inputs(c, x, cvec, w_mod, b_mod, norm_mix, norm_mlp, pool_scale, w_pool, w_up, w_down, layer=0):
    b, half, blks = core_blocks(c)
    xb = x[b]
    xTh = np.zeros((D, NBLK, HALO + NT), np.float32)
    for i, j in enumerate(blks):
        s = j * NT
        if j > 0:
            xTh[:, i, :HALO] = xb[s - HALO:s].T
        xTh[:, i, HALO:] = xb[s:s + NT].T
    hval = np.ones((128, NBLK), np.float32)
    if half == 0:
        hval[:, 0] = 0.0
    invc = np.zeros((128, 8, HALO), np.float32)
    for g, w in enumerate((2, 4, 8, 16)):
        for t in range(HALO):
            cnt = min(t + 1, w) if half == 0 else w
            invc[:, 2 * g:2 * g + 2, t] = 1.0 / cnt
    nrm = np.concatenate([fm(norm_mix[layer]), fm(norm_mlp[layer]), fm(pool_scale[0])], axis=1)
    return dict(xTh=xTh, cT=fm(cvec[b]), wmod=np.ascontiguousarray(w_mod[layer]), bmod=fm(b_mod[layer]), nrm=nrm,
                hval=hval, invc=invc, wpool=np.ascontiguousarray(w_pool[0]), wup=np.ascontiguousarray(w_up[layer]),
                wdn=np.ascontiguousarray(w_down[layer]))

NSLOT = 32; LF = 1792; BIG = 30000.0
import ml_dtypes
BF = ml_dtypes.bfloat16

def t5_onehot(half):
    import jax, jax.numpy as jnp, math
    d = np.arange(LF) - 255 - 256 * (1 - half)
    with jax.default_device(jax.devices('cpu')[0]):
        dist = jnp.maximum(jnp.asarray(d, jnp.int32), 0)
        nf = jnp.maximum(dist, 1).astype(jnp.float32)
        large = 16 + (jnp.log(nf / 16) / math.log(1024 / 16) * 16).astype(jnp.int32)
        large = jnp.minimum(large, 31)
        bucket = np.asarray(jnp.where(dist < 16, dist, large))
    oh = np.zeros((33, LF), np.float32)
    for y in range(LF):
        if d[y] >= 0:
            oh[bucket[y], y] = 1.0
        else:
            oh[32, y] = -BIG
    return oh

def p3_consts(half):
    pm = np.zeros((NBLK, NSLOT), np.float32); cfar = np.zeros_like(pm); cown = np.zeros_like(pm)
    for i in range(NBLK):
        own_u = 2 * i + 1
        own_t = 2 * i + half
        for s in range(NSLOT):
            if s == own_t:
                pm[i, s] = -2 * BIG; cown[i, s] = BIG
            elif s > own_t:
                pm[i, s] = -BIG
                if s <= own_u: cown[i, s] = -BIG
            else:
                if own_u - s >= 6: cfar[i, s] = 1.0
    rep = lambda a: np.ascontiguousarray(np.broadcast_to(a[None], (128,) + a.shape))
    ind = np.zeros((32, NSLOT * NT), np.float32)
    for s in range(NSLOT): ind[s, s * NT:(s + 1) * NT] = 1.0
    jm = np.zeros((128, 256), np.float32)
    jm[np.arange(128), 127 - np.arange(128)] = 1.0
    jm[np.arange(128), 128 + np.arange(128)] = 1.0
    return dict(pm=rep(pm), cfar=rep(cfar), cown=rep(cown), ind=ind.astype(BF), jmat=jm.astype(BF))

def p3_inputs(c, qT_own, kT_pair, v_pair, km_pair, rel_bias, consts, onehots):
    b, half, blks = core_blocks(c)
    kTs = np.zeros((D, NSLOT * NT), BF); vS = np.zeros((NSLOT * NT, D), BF); kmS = np.zeros((D, NSLOT), np.float32)
    for j in range(32):
        r, l = j % 2, j // 2
        kTs[:, j * NT:(j + 1) * NT] = kT_pair[r][:, l * NT:(l + 1) * NT]
        vS[j * NT:(j + 1) * NT] = v_pair[r][l * NT:(l + 1) * NT]
        kmS[:, j] = km_pair[r][:, l]
    rbaug = np.concatenate([rel_bias, np.ones((1, 16), np.float32)], 0).astype(np.float32)
    rb31 = np.ascontiguousarray(np.broadcast_to(rel_bias[31][None], (128, 16))).astype(np.float32)
    d = dict(qT=qT_own, kTs=kTs, vS=vS, kmS=kmS, rb31=rb31, rbaug=rbaug, onehot=onehots[half])
    d.update(consts[half])
    return d


def kernel(x, c, rel_bias, w_mod, b_mod, norm_mix, norm_mlp, w_pool, pool_scale, w_qkv, w_o, w_up, w_down, norm_final):
    f32 = lambda a: np.ascontiguousarray(np.asarray(a, dtype=np.float32))
    x, c, rel_bias, w_mod, b_mod = f32(x), f32(c), f32(rel_bias), f32(w_mod), f32(b_mod)
    norm_mix, norm_mlp, w_pool, pool_scale = f32(norm_mix), f32(norm_mlp), f32(w_pool), f32(pool_scale)
    w_qkv, w_o, w_up, w_down, norm_final = f32(w_qkv), f32(w_o), f32(w_up), f32(w_down), f32(norm_final)
    NC = 8
    cores = list(range(NC))
    nc = build_fused()
    consts = [p3_consts(0), p3_consts(1)]
    onehots = [t5_onehot(0), t5_onehot(1)]
    rbaug = np.concatenate([rel_bias, np.ones((1, 16), np.float32)], 0).astype(np.float32)
    rb31 = np.ascontiguousarray(np.broadcast_to(rel_bias[31][None], (128, 16))).astype(np.float32)
    shared = dict(wmod0=np.ascontiguousarray(w_mod[0]), wmod1=np.ascontiguousarray(w_mod[1]), bmod0=fm(b_mod[0]), bmod1=fm(b_mod[1]),
                  nrm2=fm(norm_mix[1]), nrm4=np.concatenate([fm(norm_mlp[1]), fm(norm_final)], axis=1),
                  wpool=np.ascontiguousarray(w_pool[0]), wup0=np.ascontiguousarray(w_up[0]), wdn0=np.ascontiguousarray(w_down[0]),
                  wqkv=np.ascontiguousarray(w_qkv[0]), wo=np.ascontiguousarray(w_o[0]), wup1=np.ascontiguousarray(w_up[1]),
                  wdn1=np.ascontiguousarray(w_down[1]), rb31=rb31, rbaug=rbaug)
    in_maps = []
    for cc in cores:
        b, half, blks = core_blocks(cc)
        p1 = p1_inputs(cc, x, c, w_mod, b_mod, norm_mix, norm_mlp, pool_scale, w_pool, w_up, w_down)
        d = dict(shared)
        d.update(xTh=p1["xTh"], cT=p1["cT"], nrm1=p1["nrm"], hval=p1["hval"], invc=p1["invc"], onehot=onehots[half])
        d.update(consts[half])
        in_maps.append(d)
    res = run_bass_kernel_spmd(nc, in_maps, core_ids=cores).results
    out = np.zeros(x.shape, np.float32)
    for cc in cores:
        b, half, blks = core_blocks(cc)
        oTc = np.asarray(res[cc]["outT"])
        for i, j in enumerate(blks):
            out[b, j * NT:(j + 1) * NT, :] = oTc[:, i * NT:(i + 1) * NT].T
    return out
```
